# Optimizing a Trainium2 kernel written in Bass

```python
import math
import jax, jax.numpy as jnp
from jax import lax
import numpy as np


D_MODEL = 1024
BATCH = 4
SEQ = 4096
DEPTH = 1
DEC_BATCH = 4
DEC_SEQ = 8192
PAST_LEN = 128

D_MIX = D_MODEL
D_HYENA = D_MIX // 2
D_RET = D_MIX - D_HYENA
HYENA_ORDER = 2
N_RET_HEADS = 4
RET_HEAD_DIM = D_RET // N_RET_HEADS
RET_CHUNK = 128
D_FF = 2816
FILT_EMB = 33
FILT_BANDS = (FILT_EMB - 1) // 2
FILT_HIDDEN = 64
ROPE_BASE = 10000.0
NORM_EPS = 1e-6
HYENA_TARGET = 1e-2
FAST_DECAY_PCT = 0.3
SLOW_DECAY_PCT = 1.5
D_IN = (HYENA_ORDER + 1) * D_HYENA + 4 * D_RET

kernel_name = 'hybrid_hyena_retention_encoder'


def rmsnorm(x, g):
    xf = x.astype(jnp.float32)
    y = xf * lax.rsqrt(jnp.mean(xf * xf, axis=-1, keepdims=True) + NORM_EPS)
    return (y * g.astype(jnp.float32)).astype(x.dtype)


def swiglu(h, w1, w3, w2):
    return (jax.nn.silu(h @ w1) * (h @ w3)) @ w2


def short_conv(u, w, b):
    L = u.shape[1]
    up = jnp.pad(u, ((0, 0), (1, 1), (0, 0)))
    return up[:, :L] * w[0] + up[:, 1:L + 1] * w[1] + up[:, 2:] * w[2] + b


def hyena_filters(L, w1, b1, w2, b2, w3, b3, w4, freq):
    f32 = jnp.float32
    t = jnp.linspace(0.0, 1.0, L, dtype=f32)[:, None]
    w = 2.0 * math.pi * jnp.arange(L, dtype=f32)[:, None] / L
    fb = jnp.linspace(1e-4, FILT_BANDS - 1, FILT_BANDS, dtype=f32)[None, :]
    z = jnp.concatenate([t, jnp.cos(fb * w), -jnp.sin(fb * w)], axis=-1)
    fr = freq.astype(f32)
    h = jnp.sin(fr * (z @ w1.astype(f32) + b1.astype(f32)))
    h = jnp.sin(fr * (h @ w2.astype(f32) + b2.astype(f32)))
    h = jnp.sin(fr * (h @ w3.astype(f32) + b3.astype(f32)))
    h = h @ w4.astype(f32)
    min_decay = math.log(HYENA_TARGET) / SLOW_DECAY_PCT
    max_decay = math.log(HYENA_TARGET) / FAST_DECAY_PCT
    deltas = jnp.linspace(min_decay, max_decay, D_HYENA, dtype=f32)
    window = jnp.exp(-t * jnp.abs(deltas)[None, :])
    h = h.reshape(L, HYENA_ORDER, 2, D_HYENA) * window[:, None, None, :]
    h = h / jnp.sum(jnp.abs(h), axis=0, keepdims=True)
    return jnp.transpose(h, (1, 2, 0, 3))


def bidir_long_conv(z, hf, hb):
    L = z.shape[1]
    k = jnp.concatenate([hf, jnp.zeros((1, hf.shape[1]), hf.dtype), hb[1:][::-1]], axis=0)
    Z = jnp.fft.rfft(z, n=2 * L, axis=1)
    K = jnp.fft.rfft(k, n=2 * L, axis=0)
    return jnp.fft.irfft(Z * K[None], n=2 * L, axis=1)[:, :L]


def hyena_mixer(u, sw, sb, fw1, fb1, fw2, fb2, fw3, fb3, fw4, freq, bias):
    L = u.shape[1]
    u = short_conv(u, sw, sb).astype(jnp.float32)
    v = u[..., :D_HYENA]
    gates = (u[..., D_HYENA:2 * D_HYENA], u[..., 2 * D_HYENA:])
    hs = hyena_filters(L, fw1, fb1, fw2, fb2, fw3, fb3, fw4, freq)
    bias = bias.astype(jnp.float32)
    z = v
    for n in range(HYENA_ORDER):
        z = gates[n] * (bidir_long_conv(z, hs[n, 0], hs[n, 1]) + bias[n] * z)
    return z


def rotary(x):
    L, d = x.shape[2], x.shape[3]
    inv = 1.0 / (ROPE_BASE ** (jnp.arange(0, d, 2, dtype=jnp.float32) / d))
    ang = jnp.arange(L, dtype=jnp.float32)[:, None] * inv[None, :]
    c, s = jnp.cos(ang), jnp.sin(ang)
    x1, x2 = x[..., :d // 2], x[..., d // 2:]
    return jnp.concatenate([x1 * c - x2 * s, x1 * s + x2 * c], axis=-1)


def retention_one_dir(q, k, v, lg, inclusive):
    B, H, L, d = q.shape
    C = RET_CHUNK
    n = L // C
    idx = jnp.arange(C, dtype=jnp.float32)
    diff = idx[:, None] - idx[None, :]
    mask = (diff >= 0) if inclusive else (diff > 0)
    Dm = jnp.where(mask[None], jnp.exp(jnp.maximum(diff, 0.0)[None] * lg[:, None, None]), 0.0)
    qc = q.reshape(B, H, n, C, d)
    kc = k.reshape(B, H, n, C, d)
    vc = v.reshape(B, H, n, C, d)
    s = jnp.einsum('bhncd,bhnjd->bhncj', qc, kc) * Dm[None, :, None]
    intra = jnp.einsum('bhncj,bhnje->bhnce', s, vc)
    wk = jnp.exp((C - 1 - idx)[None, :] * lg[:, None])
    wq = jnp.exp((idx + 1)[None, :] * lg[:, None])
    gC = jnp.exp(C * lg)
    kw = kc * wk[None, :, None, :, None]
    xs = (jnp.moveaxis(kw, 2, 0), jnp.moveaxis(vc, 2, 0))

    def step(R, kv):
        kk, vv = kv
        Rn = R * gC[None, :, None, None] + jnp.einsum('bhcd,bhce->bhde', kk, vv)
        return Rn, R

    _, states = lax.scan(step, jnp.zeros((B, H, d, d), jnp.float32), xs)
    cross = jnp.einsum('bhncd,nbhde->bhnce', qc * wq[None, :, None, :, None], states)
    return (intra + cross).reshape(B, H, L, d)


def retention_mixer(u, lg_f, lg_b):
    B, L, _ = u.shape
    u = u.astype(jnp.float32)

    def heads(t):
        return t.reshape(B, L, N_RET_HEADS, RET_HEAD_DIM).transpose(0, 2, 1, 3)

    q = rotary(heads(u[..., :D_RET]))
    k = rotary(heads(u[..., D_RET:2 * D_RET])) * (RET_HEAD_DIM ** -0.5)
    v = heads(u[..., 2 * D_RET:3 * D_RET])
    g = u[..., 3 * D_RET:]
    lg_f = lg_f.astype(jnp.float32)
    lg_b = lg_b.astype(jnp.float32)
    o_f = retention_one_dir(q, k, v, lg_f, True)
    o_b = jnp.flip(retention_one_dir(jnp.flip(q, 2), jnp.flip(k, 2), jnp.flip(v, 2), lg_b, False), 2)
    o = o_f + o_b
    o = o * lax.rsqrt(jnp.mean(o * o, axis=-1, keepdims=True) + NORM_EPS)
    o = o.transpose(0, 2, 1, 3).reshape(B, L, D_RET)
    return jax.nn.silu(g) * o


def _layer(x, p):
    h = rmsnorm(x, p['ffn1_pre_g'])
    x = x + 0.5 * rmsnorm(swiglu(h, p['ffn1_w1'], p['ffn1_w3'], p['ffn1_w2']), p['ffn1_post_g'])
    h = rmsnorm(x, p['mix_pre_g'])
    u = h @ p['w_in']
    nh = (HYENA_ORDER + 1) * D_HYENA
    y_h = hyena_mixer(u[..., :nh], p['short_w'], p['short_b'], p['filt_w1'], p['filt_b1'],
                      p['filt_w2'], p['filt_b2'], p['filt_w3'], p['filt_b3'], p['filt_w4'],
                      p['filt_freq'], p['hyena_bias'])
    y_r = retention_mixer(u[..., nh:], p['ret_log_decay_f'], p['ret_log_decay_b'])
    y = jnp.concatenate([y_h, y_r], axis=-1).astype(x.dtype) @ p['w_out']
    x = x + rmsnorm(y, p['mix_post_g'])
    h = rmsnorm(x, p['ffn2_pre_g'])
    x = x + 0.5 * rmsnorm(swiglu(h, p['ffn2_w1'], p['ffn2_w3'], p['ffn2_w2']), p['ffn2_post_g'])
    return x


def setup_inputs(seed: int = 0) -> dict:
    key = jax.random.key(seed)
    ks = jax.random.split(key, 32)
    f32 = jnp.float32

    def nrm(k, shape, scale):
        return jax.random.normal(k, shape, f32) * scale

    def gain(k):
        return 1.0 + 0.02 * jax.random.normal(k, (DEPTH, D_MODEL), f32)

    base = jnp.log(1.0 - 2.0 ** (-5.0 - jnp.arange(N_RET_HEADS, dtype=f32)))
    return {
        'x_prompt': jax.random.normal(ks[0], (BATCH, SEQ, D_MODEL), f32),
        'x_sample': jax.random.normal(ks[1], (DEC_BATCH, DEC_SEQ, D_MODEL), f32),
        'ffn1_pre_g': gain(ks[2]),
        'ffn1_w1': nrm(ks[3], (DEPTH, D_MODEL, D_FF), D_MODEL ** -0.5),
        'ffn1_w3': nrm(ks[4], (DEPTH, D_MODEL, D_FF), D_MODEL ** -0.5),
        'ffn1_w2': nrm(ks[5], (DEPTH, D_FF, D_MODEL), D_FF ** -0.5),
        'ffn1_post_g': gain(ks[6]),
        'mix_pre_g': gain(ks[7]),
        'w_in': nrm(ks[8], (DEPTH, D_MODEL, D_IN), D_MODEL ** -0.5),
        'short_w': nrm(ks[9], (DEPTH, 3, (HYENA_ORDER + 1) * D_HYENA), 3 ** -0.5),
        'short_b': nrm(ks[10], (DEPTH, (HYENA_ORDER + 1) * D_HYENA), 0.02),
        'filt_w1': nrm(ks[11], (DEPTH, FILT_EMB, FILT_HIDDEN), FILT_EMB ** -0.5),
        'filt_b1': nrm(ks[12], (DEPTH, FILT_HIDDEN), 0.1),
        'filt_w2': nrm(ks[13], (DEPTH, FILT_HIDDEN, FILT_HIDDEN), FILT_HIDDEN ** -0.5),
        'filt_b2': nrm(ks[14], (DEPTH, FILT_HIDDEN), 0.1),
        'filt_w3': nrm(ks[15], (DEPTH, FILT_HIDDEN, FILT_HIDDEN), FILT_HIDDEN ** -0.5),
        'filt_b3': nrm(ks[16], (DEPTH, FILT_HIDDEN), 0.1),
        'filt_w4': nrm(ks[17], (DEPTH, FILT_HIDDEN, HYENA_ORDER * 2 * D_HYENA), FILT_HIDDEN ** -0.5),
        'filt_freq': 1.0 + 0.1 * jax.random.normal(ks[18], (DEPTH, FILT_HIDDEN), f32),
        'hyena_bias': nrm(ks[19], (DEPTH, HYENA_ORDER, D_HYENA), 0.1),
        'ret_log_decay_f': base[None, :] * (1.0 + 0.02 * jax.random.normal(ks[20], (DEPTH, N_RET_HEADS), f32)),
        'ret_log_decay_b': base[None, :] * (1.0 + 0.02 * jax.random.normal(ks[21], (DEPTH, N_RET_HEADS), f32)),
        'w_out': nrm(ks[22], (DEPTH, D_MIX, D_MODEL), D_MIX ** -0.5),
        'mix_post_g': gain(ks[23]),
        'ffn2_pre_g': gain(ks[24]),
        'ffn2_w1': nrm(ks[25], (DEPTH, D_MODEL, D_FF), D_MODEL ** -0.5),
        'ffn2_w3': nrm(ks[26], (DEPTH, D_MODEL, D_FF), D_MODEL ** -0.5),
        'ffn2_w2': nrm(ks[27], (DEPTH, D_FF, D_MODEL), D_FF ** -0.5),
        'ffn2_post_g': gain(ks[28]),
    }


def reference(x_prompt, x_sample, ffn1_pre_g, ffn1_w1, ffn1_w3, ffn1_w2, ffn1_post_g,
              mix_pre_g, w_in, short_w, short_b, filt_w1, filt_b1, filt_w2, filt_b2,
              filt_w3, filt_b3, filt_w4, filt_freq, hyena_bias, ret_log_decay_f,
              ret_log_decay_b, w_out, mix_post_g, ffn2_pre_g, ffn2_w1, ffn2_w3, ffn2_w2,
              ffn2_post_g):
    def run(x):
        for l in range(DEPTH):
            p = dict(ffn1_pre_g=ffn1_pre_g[l], ffn1_w1=ffn1_w1[l], ffn1_w3=ffn1_w3[l],
                     ffn1_w2=ffn1_w2[l], ffn1_post_g=ffn1_post_g[l], mix_pre_g=mix_pre_g[l],
                     w_in=w_in[l], short_w=short_w[l], short_b=short_b[l],
                     filt_w1=filt_w1[l], filt_b1=filt_b1[l], filt_w2=filt_w2[l],
                     filt_b2=filt_b2[l], filt_w3=filt_w3[l], filt_b3=filt_b3[l],
                     filt_w4=filt_w4[l], filt_freq=filt_freq[l], hyena_bias=hyena_bias[l],
                     ret_log_decay_f=ret_log_decay_f[l], ret_log_decay_b=ret_log_decay_b[l],
                     w_out=w_out[l], mix_post_g=mix_post_g[l], ffn2_pre_g=ffn2_pre_g[l],
                     ffn2_w1=ffn2_w1[l], ffn2_w3=ffn2_w3[l], ffn2_w2=ffn2_w2[l],
                     ffn2_post_g=ffn2_post_g[l])
            x = _layer(x, p)
        return x

    y_prompt = run(x_prompt)
    y_sample = run(x_sample)
    return (y_prompt, y_sample)
```

```python
import math
import numpy as np
import ml_dtypes
from contextlib import ExitStack
import concourse.bass as bass
import concourse.mybir as mybir
from concourse.bass_utils import run_bass_kernel_spmd

F32 = mybir.dt.float32
BF16 = mybir.dt.bfloat16
AF = mybir.ActivationFunctionType
ALU = mybir.AluOpType
AX = mybir.AxisListType

EPOCH = 12000
ARENA_BYTES = 212832
ENGS = ("pe", "act", "dve", "pool", "sp")

D = 1024
KD = 8
FF = 2816
KF = 22
T = 512
TT = 4
L = 8192
NB = L // T
NHC = 12
EPS = 1e-6


class Tl:
    __slots__ = ("h", "name", "lw", "rd", "lwx")

    def __init__(self, h, name=""):
        self.h = h
        self.name = name
        self.lw = {}
        self.lwx = None
        self.rd = {}

    def __getitem__(self, k):
        return self.h[k]


class Op:
    __slots__ = ("fn", "inc", "lane", "idx")

    def __init__(self, fn, lane, idx):
        self.fn = fn
        self.lane = lane
        self.idx = idx
        self.inc = False


class Prog:
    def __init__(self, nc, n_sp_lanes=14, n_pool_lanes=8):
        self.nc = nc
        self.es = ExitStack()
        self.streams = {e: [] for e in ENGS}
        self.lanes = {e: [] for e in ENGS}
        self.clock = {}
        self.known = {e: {} for e in ENGS}
        self.dma_lanes = {"sp": [f"sp{i}" for i in range(n_sp_lanes)],
                          "pool": [f"pl{i}" for i in range(n_pool_lanes)]}
        self.dma_rr = {"sp": 0, "pool": 0}
        for q in self.dma_lanes.values():
            for l in q:
                self.lanes[l] = []
        self.nops = 0
        self.nalloc = 0
        self.scopes = []
        self.arena = None
        self.arena_off = 0
        self.peak = []

    def sbuf(self, name, shape, dt):
        if self.arena is None:
            self.arena = self.es.enter_context(self.nc.sbuf_tensor("arena", [128, ARENA_BYTES], mybir.dt.uint8))
        isz = 4 if dt == F32 else 2
        n = 1
        for d_ in shape[1:]:
            n *= d_
        nbytes = (n * isz + 31) // 32 * 32
        off = self.arena_off
        assert off + nbytes <= ARENA_BYTES, f"SBUF arena overflow allocating {name} {shape}: {off}+{nbytes}"
        self.arena_off += nbytes
        ap = self.arena[0:shape[0], off:off + n * isz].bitcast(dt)
        if len(shape) == 3:
            ap = ap.rearrange("p (a b) -> p a b", a=shape[1])
        elif len(shape) == 4:
            ap = ap.rearrange("p (a b c) -> p a b c", a=shape[1], b=shape[2])
        return Tl(ap, name)

    def push_scope(self):
        self.scopes.append(self.arena_off)

    def pop_scope(self):
        self.barrier()
        self.peak.append(self.arena_off)
        self.arena_off = self.scopes.pop()

    def barrier(self):
        deps = [(l, len(ops) - 1) for l, ops in self.lanes.items() if ops]
        for e in ENGS:
            self._wait_for(e, deps)

    def psum(self, name, shape, dt=F32):
        return Tl(self.es.enter_context(self.nc.psum_tensor(name, list(shape), dt)), name)

    def dram(self, name, shape, dt, kind="Internal"):
        return Tl(self.nc.dram_tensor(name, list(shape), dt, kind=kind).ap(), name)

    def _deps(self, reads, writes, pe_acc, djw=()):
        deps = []
        for t in reads:
            deps.extend(t.lw.items())
        for t in writes:
            for l, i in t.lw.items():
                if not (pe_acc and l == "pe"):
                    deps.append((l, i))
            for l, i in t.rd.items():
                if not (pe_acc and l == "pe"):
                    deps.append((l, i))
        for t in djw:
            if t.lwx is not None:
                deps.append(t.lwx)
            deps.extend(t.rd.items())
        return deps

    def _wait_for(self, eng, deps):
        kn = self.known[eng]
        for (l, i) in sorted(set(deps), key=lambda d: -d[1]):
            if kn.get(l, -1) >= i:
                continue
            self.lanes[l][i].inc = True
            self.streams[eng].append(("w", l, i))
            for l2, i2 in self.clock[(l, i)].items():
                if kn.get(l2, -1) < i2:
                    kn[l2] = i2

    def _finish(self, lane, idx, eng, reads, writes, djw=()):
        c = dict(self.known[eng])
        c[lane] = idx
        self.clock[(lane, idx)] = c
        for t in reads:
            t.rd[lane] = idx
        for t in writes:
            t.lw = {lane: idx}
            t.lwx = (lane, idx)
            t.rd = {}
        for t in djw:
            t.lw[lane] = idx

    def op(self, eng, fn, reads=(), writes=(), pe_acc=False, djw=()):
        self._wait_for(eng, self._deps(reads, writes, pe_acc, djw))
        idx = len(self.lanes[eng])
        o = Op(fn, eng, idx)
        self.lanes[eng].append(o)
        self.streams[eng].append(("o", o))
        self._finish(eng, idx, eng, reads, writes, djw)
        self.nops += 1
        return o

    def dma(self, q, out, in_, reads=(), writes=(), djw=(), **kw):
        ls = self.dma_lanes[q]
        lane = ls[self.dma_rr[q] % len(ls)]
        self.dma_rr[q] += 1
        deps = self._deps(reads, writes, False, djw)
        idx = len(self.lanes[lane])
        if idx > 0:
            deps.append((lane, idx - 1))
        self._wait_for(q, deps)
        o = Op(lambda e: e.dma_start(out=out, in_=in_, **kw), lane, idx)
        o.inc = True
        self.lanes[lane].append(o)
        self.streams[q].append(("o", o))
        self._finish(lane, idx, q, reads, writes, djw)
        self.nops += 1
        return o

    def finish_all(self, q="sp"):
        deps = []
        for qq, ls in self.dma_lanes.items():
            for l in ls:
                if self.lanes[l]:
                    deps.append((l, len(self.lanes[l]) - 1))
        self._wait_for(q, deps)

    def emit(self):
        nc = self.nc
        cnt = {}
        sems = {}
        for l, ops in self.lanes.items():
            c = 0
            for o in ops:
                if o.inc:
                    c += 1
                    cnt[(l, o.idx)] = c
            for e in range((c + EPOCH - 1) // EPOCH):
                sems[(l, e)] = self.es.enter_context(nc.semaphore(f"s_{l}_{e}"))
        self.n_sems = len(sems)

        def isdma(l):
            return l not in ENGS

        def run(engobj, items):
            for it in items:
                if it[0] == "w":
                    c = cnt[(it[1], it[2])] - 1
                    v = c % EPOCH + 1
                    engobj.wait_ge(sems[(it[1], c // EPOCH)], v * 16 if isdma(it[1]) else v)
                else:
                    o = it[1]
                    ins = o.fn(engobj)
                    if o.inc:
                        c = cnt[(o.lane, o.idx)] - 1
                        ins.then_inc(sems[(o.lane, c // EPOCH)], 16 if isdma(o.lane) else 1)

        with nc.Block() as block:
            @block.sync
            def _(e):
                run(e, self.streams["sp"])

            @block.tensor
            def _(e):
                run(e, self.streams["pe"])

            @block.scalar
            def _(e):
                run(e, self.streams["act"])

            @block.vector
            def _(e):
                run(e, self.streams["dve"])

            @block.gpsimd
            def _(e):
                run(e, self.streams["pool"])
        self.es.close()


class Ring:
    def __init__(self, tiles):
        self.t = tiles
        self.i = 0

    def next(self):
        t = self.t[self.i % len(self.t)]
        self.i += 1
        return t


def build(phases=("A", "B", "C", "D"), nb=NB, debug_outs=(), ext_in=()):
    nc = bass.Bass("TRN2", target_bir_lowering=False)
    P = Prog(nc)
    dbg = set(debug_outs)

    def din(name, shape, dt=F32):
        return P.dram(name, shape, dt, kind="ExternalInput")

    def dscr(name, shape, dt):
        kind = "ExternalOutput" if name in dbg else ("ExternalInput" if name in ext_in else "Internal")
        return P.dram(name, shape, dt, kind=kind)

    x_in = din("x", [L, D])
    ident_d = din("ident", [128, 128], BF16)
    w13_d = [din(f"w13_{i}", [KF, 128, 2 * KD * 128]) for i in (1, 2)]
    w2_d = [din(f"w2_{i}", [128, KF * D]) for i in (1, 2)]
    winfm_d = din("win_fm", [20, 128, KD * 128])
    wintm_d = din("win_tm", [2, 128, KD * 512])
    wout_d = din("wout", [128, 8 * D])
    gpre_d = {k: din(k, [128, KD]) for k in ("ffn1_pre_g", "mix_pre_g", "ffn2_pre_g")}
    gpost_d = {k: din(k, [1, D]) for k in ("ffn1_post_g", "mix_post_g", "ffn2_post_g")}
    shortw_d = din("short_w", [128, NHC, 3])
    shortb_d = din("short_b", [128, NHC])
    rot_d = din("rot", [NB, 128, 2, T])
    pswap_d = din("pswap", [128, 128])
    flag_d = din("pflag", [128, 1])
    zext_d = din("zext", [33, 2, 16384], BF16)
    text_d = din("text", [1, 16384])
    negd_d = din("negd", [128, 4])
    fw1_d = din("filt_w1", [33, 64])
    fw2_d = din("filt_w2", [64, 64])
    fw3_d = din("filt_w3", [64, 64])
    fw4_d = din("filt_w4", [64, 2048])
    fb_d = din("filt_b", [64, 3])
    ffr_d = din("filt_freq", [64, 1])
    hbrow_d = din("hbias_row", [1, 1024])
    F1d_d = din("F1d", [64, 256], BF16)
    F1k_d = din("F1k", [128, 256], BF16)
    GT_d = din("GT", [128, 128, 384], BF16)
    F3_d = din("F3", [128, 2, 256], BF16)
    GI_d = din("GI", [128, 128, 128], BF16)
    lgf_d = din("ret_log_decay_f", [1, 4])
    lgb_d = din("ret_log_decay_b", [1, 4])
    rtab_d = din("rtab", [128, 4, 128])
    itab_d = din("itab", [128, 2, 128])
    jtab_d = din("jtab", [128, 3])

    out_d = P.dram("out", [L, D], F32, kind="ExternalOutput")
    w13b = [dscr(f"w13b_{i}", [KF, 128, 2 * KD * 128], BF16) for i in (1, 2)]
    winfmb = dscr("winfmb", [20, 128, KD * 128], BF16)
    X1 = dscr("X1", [L, D], F32)
    UC = dscr("UC", [NHC * 128, L + 1], F32)
    QT = dscr("QT", [4, 128, L], BF16)
    KTs = dscr("KT", [4, 128, L], BF16)
    KM = dscr("KM", [4, 128, L // 128, 128], BF16)
    VM = dscr("VM", [4, 128, L // 128, 128], BF16)
    GM = dscr("GM", [L, 512], F32)
    YT = dscr("YT", [D, L], BF16)
    X2 = dscr("X2", [L, D], F32)
    Z1F = dscr("Z1F", [512, L], F32)
    KTM = dscr("KTM", [2, 512, 16384], BF16)
    KSP = dscr("KSP", [2, 4, 32, 128, 2, 512], BF16)

    ident = P.sbuf("ident_s", [128, 128], BF16)
    P.dma("sp", ident[:], ident_d[:], reads=[ident_d], writes=[ident])
    gpre = {}
    for k, v in gpre_d.items():
        gpre[k] = P.sbuf(k + "_s", [128, KD], F32)
        P.dma("sp", gpre[k][:], v[:], reads=[v], writes=[gpre[k]])
    def load_gpost(k, half):
        t = P.sbuf(k + "_s", [128, D], F32)
        P.dma("sp", t[:], gpost_d[k].h.partition_broadcast(128).rearrange("p o d -> p (o d)"), reads=[gpost_d[k]], writes=[t])
        if half:
            P.op("pool", lambda e: e.tensor_scalar(out=t[:], in0=t[:], scalar1=0.5, scalar2=None, op0=ALU.mult), reads=[t], writes=[t])
        return t

    banks = [P.psum(f"bank{i}", [128, 512], F32) for i in range(8)]
    rot4 = Ring(banks[0:4])
    ysets = [(banks[4], banks[5]), (banks[6], banks[7])]

    xring = hnring = hn2ring = hTring = aT = w13ring = w2s = sgring = tmpring = xoring = None
    junk = P.sbuf("junk", [128, D], BF16)
    string = Ring([P.sbuf(f"st{i}", [128, 16], F32) for i in range(3)])
    s2ring = Ring([P.sbuf(f"s2{i}", [128, 8], F32) for i in range(4)])

    def alloc_ffn():
        nonlocal xring, hnring, hn2ring, hTring, aT, w13ring, w2s, sgring, tmpring, xoring
        xring = Ring([P.sbuf(f"xr{i}", [128, D], F32) for i in range(3)])
        hnring = Ring([P.sbuf(f"hn{i}", [128, D], BF16) for i in range(4)])
        hn2ring = Ring([P.sbuf(f"hnb{i}", [128, D], BF16) for i in range(4)])
        hTring = Ring([P.sbuf(f"hT{i}", [128, KD, T], BF16) for i in range(3)])
        aT = [P.sbuf(f"aT{i}", [128, T], BF16) for i in range(KF)]
        w13ring = Ring([P.sbuf(f"w13s{i}", [128, 2, KD, 128], BF16) for i in range(3)])
        w2s = P.sbuf("w2s", [128, KF, D], BF16)
        sgring = Ring([P.sbuf(f"sg{i}", [128, T], F32) for i in range(2)])
        tmpring = Ring([P.sbuf(f"tmp{i}", [128, D], F32) for i in range(1)])
        xoring = Ring([P.sbuf(f"xo{i}", [128, D], F32) for i in range(2)])

    def cast_weights(i):
        for fc in range(KF):
            P.dma("pool", w13b[i][fc], w13_d[i][fc], reads=[w13_d[i]], writes=[w13b[i]])

    def load_w2(i):
        for j in range(KF):
            P.dma("pool", w2s[:, j, :], w2_d[i][:, j * D:(j + 1) * D], reads=[w2_d[i]], writes=[w2s])

    def prep_tile(xt, st, col, gkey):
        P.op("act", lambda e: e.activation(out=junk[:], in_=xt[:], func=AF.Square, accum_out=st[:, col:col + 1]),
             reads=[xt], writes=[junk, st])

    def rstd_cols(st, c0, n):
        P.op("act", lambda e: e.activation(out=st[:, 8 + c0:8 + c0 + n], in_=st[:, c0:c0 + n], func=AF.Sqrt, bias=EPS, scale=1.0 / D),
             reads=[st], writes=[st])
        P.op("dve", lambda e: e.reciprocal(out=st[:, 8 + c0:8 + c0 + n], in_=st[:, 8 + c0:8 + c0 + n]), reads=[st], writes=[st])

    def norm_cast(xt, st, col, ring=None):
        hn = (ring or hnring).next()
        P.op("act", lambda e: e.activation(out=hn[:], in_=xt[:], func=AF.Copy, scale=st[:, 8 + col:9 + col]),
             reads=[xt, st], writes=[hn])
        return hn

    def transpose_tile(hn, hT, tt, g):
        bk = rot4.next()
        bv = bk[:].bitcast(BF16)
        for kd in range(KD):
            P.op("pe", lambda e, kd=kd: e.transpose(out=bv[:, kd * 128:(kd + 1) * 128], in_=hn[:, kd * 128:(kd + 1) * 128], identity=ident[:]),
                 reads=[hn, ident], writes=[bk], pe_acc=True)
        P.op("dve", lambda e: e.tensor_tensor(out=hT[:, :, tt * 128:(tt + 1) * 128],
                                              in0=bv[:, 0:KD * 128].rearrange("p (k t) -> p k t", k=KD),
                                              in1=g[:].unsqueeze(2).to_broadcast([128, KD, 128]), op=ALU.mult),
             reads=[bk, g], writes=[hT])

    def ffn_s3(hT, w13bi):
        for fc in range(KF):
            ws = w13ring.next()
            P.dma("sp", ws[:].rearrange("p a k f -> p (a k f)"), w13bi[fc], reads=[w13bi], writes=[ws])
            pg = rot4.next()
            pu = rot4.next()
            for which, pb in ((0, pg), (1, pu)):
                for kd in range(KD):
                    P.op("pe", lambda e, which=which, pb=pb, kd=kd, ws=ws: e.matmul(pb[:, 0:T], lhsT=ws[:, which, kd, :], rhs=hT[:, kd, :],
                                                                                   start=(kd == 0), stop=(kd == KD - 1)),
                         reads=[ws, hT], writes=[pb], pe_acc=True)
            sg = sgring.next()
            P.op("act", lambda e, sg=sg, pg=pg: e.activation(out=sg[:], in_=pg[:, 0:T], func=AF.Silu), reads=[pg], writes=[sg])
            P.op("dve", lambda e, sg=sg, pu=pu, fc=fc: e.tensor_tensor(out=aT[fc][:], in0=sg[:], in1=pu[:, 0:T], op=ALU.mult),
                 reads=[sg, pu], writes=[aT[fc]])

    def mm_tokmajor(ys, lhs_list, rhs_fn, reads_fn):
        n = len(lhs_list)
        for dh in range(2):
            for i, lh in enumerate(lhs_list):
                P.op("pe", lambda e, dh=dh, i=i, lh=lh: e.matmul(ys[dh][:, 0:512], lhsT=lh, rhs=rhs_fn(i, dh), start=(i == 0), stop=(i == n - 1)),
                     reads=reads_fn(i), writes=[ys[dh]], pe_acc=True)

    def epilogue(ys, xres, gp, xo):
        s2 = s2ring.next()
        for dh in range(2):
            P.op("act", lambda e, dh=dh: e.activation(out=junk[:, 0:512], in_=ys[dh][:, 0:512], func=AF.Square, accum_out=s2[:, dh:dh + 1]),
                 reads=[ys[dh]], writes=[junk, s2])
        P.op("dve", lambda e: e.tensor_tensor(out=s2[:, 2:3], in0=s2[:, 0:1], in1=s2[:, 1:2], op=ALU.add), reads=[s2], writes=[s2])
        P.op("act", lambda e: e.activation(out=s2[:, 3:4], in_=s2[:, 2:3], func=AF.Sqrt, bias=EPS, scale=1.0 / D), reads=[s2], writes=[s2])
        P.op("dve", lambda e: e.reciprocal(out=s2[:, 4:5], in_=s2[:, 3:4]), reads=[s2], writes=[s2])
        tmp = tmpring.next()
        for dh in range(2):
            P.op("dve", lambda e, dh=dh: e.scalar_tensor_tensor(out=tmp[:, dh * 512:(dh + 1) * 512], in0=ys[dh][:, 0:512], scalar=s2[:, 4:5],
                                                                 in1=gp[:, dh * 512:(dh + 1) * 512], op0=ALU.mult, op1=ALU.mult),
                 reads=[ys[dh], s2, gp], writes=[tmp])
        P.op("pool", lambda e: e.tensor_tensor(out=xo[:], in0=tmp[:], in1=xres[:], op=ALU.add), reads=[tmp, xres], writes=[xo])

    def prep_pe(hns, gkey):
        hT = hTring.next()
        for tt in range(TT):
            transpose_tile(hns[tt], hT, tt, gpre[gkey])
        return hT

    if "A" in phases:
        P.push_scope()
        alloc_ffn()
        gpost1 = load_gpost("ffn1_post_g", True)
        cast_weights(0)
        cast_weights(1)
        load_w2(0)
        for c in range(20):
            P.dma("pool", winfmb[c], winfm_d[c], reads=[winfm_d], writes=[winfmb])
        wtm = P.sbuf("wtm", [128, 2, KD, 512], BF16)
        for i in range(2):
            for kd in range(KD):
                P.dma("pool", wtm[:, i, kd, :], wintm_d[i][:, kd * 512:(kd + 1) * 512], reads=[wintm_d], writes=[wtm])
        shw = P.sbuf("shw", [128, NHC, 3], F32)
        shb = P.sbuf("shb", [128, NHC], F32)
        P.dma("sp", shw[:], shortw_d[:], reads=[shortw_d], writes=[shw])
        P.dma("sp", shb[:], shortb_d[:], reads=[shortb_d], writes=[shb])
        pswap = P.sbuf("pswap_s", [128, 128], BF16)
        P.dma("pool", pswap[:], pswap_d[:], reads=[pswap_d], writes=[pswap])
        flag = P.sbuf("flag_s", [128, 1], F32)
        P.dma("sp", flag[:], flag_d[:], reads=[flag_d], writes=[flag])
        fw = P.sbuf("fw", [128, NHC, 3], F32)
        P.op("dve", lambda e: e.tensor_scalar(out=fw[:], in0=shw[:], scalar1=flag[:, 0:1], scalar2=-1.0, op0=ALU.mult, op1=ALU.mult),
             reads=[shw, flag], writes=[fw])
        saved = P.sbuf("saved", [128, NHC, 2], F32)
        P.op("pool", lambda e: e.memset(saved[:], 0.0), writes=[saved])
        wfmring = Ring([P.sbuf(f"wfm{i}", [128, KD, 128], BF16) for i in range(3)])
        Sring = Ring([P.sbuf(f"S{i}", [128, T + 2], F32) for i in range(3)])
        accring = Ring([P.sbuf(f"acc{i}", [128, T], F32) for i in range(3)])
        qsring = Ring([P.sbuf(f"qs{i}", [128, T], BF16) for i in range(2)])
        r2ring = Ring([P.sbuf(f"r2{i}", [128, T], F32) for i in range(1)])
        r1ring = Ring([P.sbuf(f"r1{i}", [128, T], F32) for i in range(1)])
        qoring = Ring([P.sbuf(f"qo{i}", [128, T], BF16) for i in range(2)])
        rotring = Ring([P.sbuf(f"rot{i}", [128, 2, T], F32) for i in range(1)])
        kmring = Ring([P.sbuf(f"km{i}", [128, TT, 128], BF16) for i in range(2)])
        vgring = Ring([P.sbuf(f"vg{i}", [128, 512], BF16) for i in range(2)])
        ggring = Ring([P.sbuf(f"gg{i}", [128, 512], F32) for i in range(1)])

        def prep_act(b):
            st = string.next()
            hns = []
            for tt in range(TT):
                xt = xring.next()
                P.dma("sp", xt[:], x_in[b * T + tt * 128: b * T + (tt + 1) * 128, :], reads=[x_in], writes=[xt])
                prep_tile(xt, st, tt, None)
                rstd_cols(st, tt, 1)
                hns.append(norm_cast(xt, st, tt))
            return hns

        rot8 = Ring(banks)

        def s5(b, h2T):
            rot4 = rot8
            rt = rotring.next()
            P.dma("sp", rt[:], rot_d[b], reads=[rot_d], writes=[rt])
            for cc in range(NHC):
                wf = wfmring.next()
                P.dma("sp", wf[:].rearrange("p k f -> p (k f)"), winfmb[cc], reads=[winfmb], writes=[wf])
                pb = rot4.next()
                for kd in range(KD):
                    P.op("pe", lambda e, kd=kd, wf=wf, pb=pb: e.matmul(pb[:, 0:T], lhsT=wf[:, kd, :], rhs=h2T[:, kd, :], start=(kd == 0), stop=(kd == KD - 1)),
                         reads=[wf, h2T], writes=[pb], pe_acc=True)
                S = Sring.next()
                P.op("act", lambda e, S=S, pb=pb: e.activation(out=S[:, 2:T + 2], in_=pb[:, 0:T], func=AF.Copy), reads=[pb], writes=[S])
                P.op("pool", lambda e, S=S, cc=cc: e.tensor_copy(out=S[:, 0:2], in_=saved[:, cc, :]), reads=[saved], writes=[S])
                acc = accring.next()
                P.op("dve", lambda e, S=S, cc=cc, acc=acc: e.tensor_scalar(out=acc[:], in0=S[:, 1:T + 1], scalar1=shw[:, cc, 1:2], scalar2=shb[:, cc:cc + 1],
                                                                         op0=ALU.mult, op1=ALU.add), reads=[S, shw, shb], writes=[acc])
                P.op("dve", lambda e, S=S, cc=cc, acc=acc: e.scalar_tensor_tensor(out=acc[:], in0=S[:, 0:T], scalar=shw[:, cc, 0:1], in1=acc[:],
                                                                                  op0=ALU.mult, op1=ALU.add), reads=[S, shw, acc], writes=[acc])
                P.op("dve", lambda e, S=S, cc=cc, acc=acc: e.scalar_tensor_tensor(out=acc[:], in0=S[:, 2:T + 2], scalar=shw[:, cc, 2:3], in1=acc[:],
                                                                                 op0=ALU.mult, op1=ALU.add), reads=[S, shw, acc], writes=[acc])
                if b == NB // 2:
                    P.op("dve", lambda e, S=S, cc=cc, acc=acc: e.scalar_tensor_tensor(out=acc[:, 1:2], in0=S[:, 2:3], scalar=fw[:, cc, 2:3], in1=acc[:, 1:2],
                                                                                     op0=ALU.mult, op1=ALU.add), reads=[S, fw, acc], writes=[acc])
                    P.op("dve", lambda e, S=S, cc=cc, acc=acc: e.scalar_tensor_tensor(out=acc[:, 2:3], in0=S[:, 1:2], scalar=fw[:, cc, 0:1], in1=acc[:, 2:3],
                                                                                     op0=ALU.mult, op1=ALU.add), reads=[S, fw, acc], writes=[acc])
                P.op("pool", lambda e, S=S, cc=cc: e.tensor_copy(out=saved[:, cc, :], in_=S[:, T:T + 2]), reads=[S], writes=[saved])
                P.dma("pool", UC[cc * 128:(cc + 1) * 128, b * T:(b + 1) * T], acc[:], reads=[acc], writes=[], djw=[UC])
            for i in range(8):
                isk = i >= 4
                h = i % 4
                wf = wfmring.next()
                P.dma("sp", wf[:].rearrange("p k f -> p (k f)"), winfmb[NHC + i], reads=[winfmb], writes=[wf])
                pb = rot4.next()
                for kd in range(KD):
                    P.op("pe", lambda e, kd=kd, wf=wf, pb=pb: e.matmul(pb[:, 0:T], lhsT=wf[:, kd, :], rhs=h2T[:, kd, :], start=(kd == 0), stop=(kd == KD - 1)),
                         reads=[wf, h2T], writes=[pb], pe_acc=True)
                qs = qsring.next()
                P.op("act", lambda e, qs=qs, pb=pb, isk=isk: e.activation(out=qs[:], in_=pb[:, 0:T], func=AF.Copy, scale=(128 ** -0.5 if isk else 1.0)),
                     reads=[pb], writes=[qs])
                pb2 = rot4.next()
                P.op("pe", lambda e, qs=qs, pb2=pb2: e.matmul(pb2[:, 0:T], lhsT=pswap[:], rhs=qs[:], start=True, stop=True),
                     reads=[pswap, qs], writes=[pb2], pe_acc=True)
                r1 = r1ring.next()
                P.op("pool", lambda e, qs=qs, r1=r1, rt=rt: e.tensor_tensor(out=r1[:], in0=qs[:], in1=rt[:, 0, :], op=ALU.mult), reads=[qs, rt], writes=[r1])
                r2 = r2ring.next()
                P.op("dve", lambda e, pb2=pb2, r2=r2, rt=rt: e.tensor_tensor(out=r2[:], in0=pb2[:, 0:T], in1=rt[:, 1, :], op=ALU.mult), reads=[pb2, rt], writes=[r2])
                qo = qoring.next()
                P.op("dve", lambda e, r1=r1, r2=r2, qo=qo: e.tensor_tensor(out=qo[:], in0=r1[:], in1=r2[:], op=ALU.add), reads=[r1, r2], writes=[qo])
                dst = KTs if isk else QT
                P.dma("pool", dst[h][:, b * T:(b + 1) * T], qo[:], reads=[qo], writes=[], djw=[dst])
                if isk:
                    bk = rot4.next()
                    bv = bk[:].bitcast(BF16)
                    for tt in range(TT):
                        P.op("pe", lambda e, tt=tt, qo=qo, bv=bv: e.transpose(out=bv[:, tt * 128:(tt + 1) * 128], in_=qo[:, tt * 128:(tt + 1) * 128], identity=ident[:]),
                             reads=[qo, ident], writes=[bk], pe_acc=True)
                    km = kmring.next()
                    P.op("act", lambda e, km=km, bv=bv: e.activation(out=km[:].rearrange("p a d -> p (a d)"), in_=bv[:, 0:TT * 128], func=AF.Copy),
                         reads=[bk], writes=[km])
                    P.dma("pool", KM[h][:, 4 * b:4 * b + 4, :], km[:], reads=[km], writes=[], djw=[KM])
            for tt in range(TT):
                for which in range(2):
                    pb = rot4.next()
                    for kd in range(KD):
                        P.op("pe", lambda e, kd=kd, pb=pb, which=which, tt=tt: e.matmul(pb[:, 0:512], lhsT=h2T[:, kd, tt * 128:(tt + 1) * 128], rhs=wtm[:, which, kd, :],
                                                                                     start=(kd == 0), stop=(kd == KD - 1)),
                             reads=[h2T, wtm], writes=[pb], pe_acc=True)
                    r0 = b * T + tt * 128
                    if which == 0:
                        vt = vgring.next()
                        P.op("act", lambda e, vt=vt, pb=pb: e.activation(out=vt[:], in_=pb[:, 0:512], func=AF.Copy), reads=[pb], writes=[vt])
                        P.dma("pool", VM[:, :, 4 * b + tt, :].rearrange("h p e -> p h e"), vt[:].rearrange("p (h e) -> p h e", h=4), reads=[vt], writes=[], djw=[VM])
                    else:
                        gt = ggring.next()
                        P.op("act", lambda e, gt=gt, pb=pb: e.activation(out=gt[:], in_=pb[:, 0:512], func=AF.Silu), reads=[pb], writes=[gt])
                        P.dma("pool", GM[r0:r0 + 128, :], gt[:], reads=[gt], writes=[], djw=[GM])

        hT_cur = prep_pe(prep_act(0), "ffn1_pre_g")
        pend = None
        for b in range(nb):
            ffn_s3(hT_cur, w13b[0])
            if pend is not None:
                s5(pend[0], prep_pe(pend[1], "mix_pre_g"))
            if b + 1 < nb:
                hns_next = prep_act(b + 1)
            st2 = string.next()
            hn2s = []
            for tt in range(TT):
                if tt == TT - 1 and b + 1 < nb:
                    hT_cur = prep_pe(hns_next, "ffn1_pre_g")
                ys = ysets[tt % 2]
                mm_tokmajor(ys, [aT[fc][:, tt * 128:(tt + 1) * 128] for fc in range(KF)],
                            lambda i, dh: w2s[:, i, dh * 512:(dh + 1) * 512], lambda i: [aT[i], w2s])
                xt = xring.next()
                r0 = b * T + tt * 128
                P.dma("sp", xt[:], x_in[r0:r0 + 128, :], reads=[x_in], writes=[xt])
                xo = xoring.next()
                epilogue(ys, xt, gpost1, xo)
                P.dma("pool", X1[r0:r0 + 128, :], xo[:], reads=[xo], writes=[], djw=[X1])
                prep_tile(xo, st2, tt, None)
                rstd_cols(st2, tt, 1)
                hn2s.append(norm_cast(xo, st2, tt, hn2ring))
            pend = (b, hn2s)
        s5(pend[0], prep_pe(pend[1], "mix_pre_g"))
        for cc in range(NHC):
            a1 = accring.next()
            P.op("dve", lambda e, cc=cc, a1=a1: e.tensor_scalar(out=a1[:, 0:1], in0=saved[:, cc, 1:2], scalar1=shw[:, cc, 1:2], scalar2=shb[:, cc:cc + 1],
                                                               op0=ALU.mult, op1=ALU.add), reads=[saved, shw, shb], writes=[a1])
            P.op("dve", lambda e, cc=cc, a1=a1: e.scalar_tensor_tensor(out=a1[:, 0:1], in0=saved[:, cc, 0:1], scalar=shw[:, cc, 0:1], in1=a1[:, 0:1],
                                                                      op0=ALU.mult, op1=ALU.add), reads=[saved, shw, a1], writes=[a1])
            P.dma("pool", UC[cc * 128:(cc + 1) * 128, nb * T:nb * T + 1], a1[:, 0:1], reads=[a1], writes=[UC], allow_slow_non_contiguous=True)
        P.pop_scope()

    if "B" in phases:
        P.push_scope()
        allb = Ring(banks)
        PI = math.pi
        P.push_scope()
        fw1 = P.sbuf("fw1", [33, 64], F32)
        fw2 = P.sbuf("fw2", [64, 64], F32)
        fw3 = P.sbuf("fw3", [64, 64], F32)
        fw4 = P.sbuf("fw4", [64, 2048], F32)
        fbs = P.sbuf("fbs", [64, 3], F32)
        ffr = P.sbuf("ffr", [64, 1], F32)
        negd = P.sbuf("negd_s", [128, 4], F32)
        for t_, d_ in ((fw1, fw1_d), (fw2, fw2_d), (fw3, fw3_d), (fw4, fw4_d), (fbs, fb_d), (ffr, ffr_d), (negd, negd_d)):
            P.dma("sp", t_[:], d_[:], reads=[d_], writes=[t_])
        frb = P.sbuf("frb", [64, 3], F32)
        P.op("dve", lambda e: e.tensor_scalar(out=frb[:], in0=fbs[:], scalar1=ffr[:, 0:1], scalar2=None, op0=ALU.mult), reads=[fbs, ffr], writes=[frb])
        h3h = P.sbuf("h3hi", [64, 16384], BF16)
        h3l = P.sbuf("h3lo", [64, 16384], BF16)

        def split(src_ap, hi_ap, lo_ap, reads, hi_t, lo_t, dj=False):
            kw_h = dict(djw=[hi_t]) if dj else dict(writes=[hi_t])
            kw_l = dict(djw=[lo_t]) if dj else dict(writes=[lo_t])
            P.op("act", lambda e: e.activation(out=hi_ap, in_=src_ap, func=AF.Copy), reads=reads, **kw_h)
            P.op("dve", lambda e: e.tensor_tensor(out=lo_ap, in0=src_ap, in1=hi_ap, op=ALU.subtract), reads=reads + [hi_t], **kw_l)

        wsp = {}
        for nm, wt, shp in (("w1", fw1, [33, 64]), ("w2", fw2, [64, 64]), ("w3", fw3, [64, 64]), ("w4", fw4, [64, 2048])):
            hi_t = P.sbuf(nm + "hi", shp, BF16)
            lo_t = P.sbuf(nm + "lo", shp, BF16)
            split(wt[:], hi_t[:], lo_t[:], [wt], hi_t, lo_t)
            wsp[nm] = (hi_t, lo_t)
        P.push_scope()
        zhring = Ring([P.sbuf(f"zh{i}", [33, 2, 512], BF16) for i in range(8)])
        hring = Ring([P.sbuf(f"hh{i}", [64, 512], F32) for i in range(8)])
        hring2 = Ring([P.sbuf(f"hg{i}", [64, 512], F32) for i in range(8)])
        hsring = Ring([P.sbuf(f"hs{i}", [64, 2, 512], BF16) for i in range(8)])

        def mm3(out_ap, wkey, c0, c1, xh, xl, reads, bk):
            wh, wl = wsp[wkey]
            P.op("pe", lambda e: e.matmul(out_ap, lhsT=wh[:, c0:c1], rhs=xh, start=True, stop=False), reads=reads + [wh], writes=[bk], pe_acc=True)
            P.op("pe", lambda e: e.matmul(out_ap, lhsT=wh[:, c0:c1], rhs=xl, start=False, stop=False), reads=reads + [wh], writes=[bk], pe_acc=True)
            P.op("pe", lambda e: e.matmul(out_ap, lhsT=wl[:, c0:c1], rhs=xh, start=False, stop=True), reads=reads + [wl], writes=[bk], pe_acc=True)

        GB = 4
        for bg in range(32 // GB):
            blks = [bg * GB + i for i in range(GB)]
            cur = []
            for blk in blks:
                zt = zhring.next()
                P.dma("sp", zt[:], zext_d[:, :, blk * 512:(blk + 1) * 512], reads=[zext_d], writes=[zt])
                cur.append((zt[:, 0, :], zt[:, 1, :], [zt]))
            for li, wkey in enumerate(("w1", "w2", "w3")):
                bks, hts, has, hbs = [], [], [], []
                for i in range(GB):
                    bk = allb.next()
                    mm3(bk[0:64, 0:512], wkey, 0, 64, cur[i][0], cur[i][1], cur[i][2], bk)
                    bks.append(bk)
                for i in range(GB):
                    ht = hring.next()
                    hts.append(ht)
                    P.op("dve", lambda e, bk=bks[i], ht=ht, li=li: e.tensor_scalar(out=ht[:], in0=bk[0:64, 0:512], scalar1=ffr[:, 0:1], scalar2=frb[:, li:li + 1], op0=ALU.mult, op1=ALU.add),
                         reads=[bks[i], ffr, frb], writes=[ht])
                for i in range(GB):
                    ha, hb_ = hring2.next(), hring2.next()
                    has.append(ha)
                    hbs.append(hb_)
                    P.op("dve", lambda e, ht=hts[i], ha=ha: e.tensor_scalar(out=ha[:], in0=ht[:], scalar1=PI, scalar2=-2.0 * PI, op0=ALU.is_gt, op1=ALU.mult), reads=[hts[i]], writes=[ha])
                    P.op("dve", lambda e, ht=hts[i], hb_=hb_: e.tensor_scalar(out=hb_[:], in0=ht[:], scalar1=-PI, scalar2=2.0 * PI, op0=ALU.is_lt, op1=ALU.mult), reads=[hts[i]], writes=[hb_])
                for i in range(GB):
                    P.op("dve", lambda e, ha=has[i], hb_=hbs[i]: e.tensor_tensor(out=ha[:], in0=ha[:], in1=hb_[:], op=ALU.add), reads=[has[i], hbs[i]], writes=[has[i]])
                for i in range(GB):
                    P.op("dve", lambda e, ht=hts[i], ha=has[i]: e.tensor_tensor(out=ht[:], in0=ht[:], in1=ha[:], op=ALU.add), reads=[hts[i], has[i]], writes=[hts[i]])
                for i in range(GB):
                    P.op("act", lambda e, ht=hts[i]: e.activation(out=ht[:], in_=ht[:], func=AF.Sin), reads=[hts[i]], writes=[hts[i]])
                nxt = []
                if li < 2:
                    hss = [hsring.next() for _ in range(GB)]
                    for i in range(GB):
                        P.op("act", lambda e, ht=hts[i], hs=hss[i]: e.activation(out=hs[:, 0, :], in_=ht[:], func=AF.Copy), reads=[hts[i]], writes=[hss[i]])
                    for i in range(GB):
                        P.op("dve", lambda e, ht=hts[i], hs=hss[i]: e.tensor_tensor(out=hs[:, 1, :], in0=ht[:], in1=hs[:, 0, :], op=ALU.subtract), reads=[hts[i], hss[i]], djw=[hss[i]])
                        nxt.append((hss[i][:, 0, :], hss[i][:, 1, :], [hss[i]]))
                    cur = nxt
                else:
                    for i, blk in enumerate(blks):
                        sl = slice(blk * 512, (blk + 1) * 512)
                        P.op("act", lambda e, ht=hts[i], sl=sl: e.activation(out=h3h[:, sl], in_=ht[:], func=AF.Copy), reads=[hts[i]], djw=[h3h])
                    for i, blk in enumerate(blks):
                        sl = slice(blk * 512, (blk + 1) * 512)
                        P.op("dve", lambda e, ht=hts[i], sl=sl: e.tensor_tensor(out=h3l[:, sl], in0=ht[:], in1=h3h[:, sl], op=ALU.subtract), reads=[hts[i], h3h], djw=[h3l])
        P.pop_scope()
        ktb = [P.sbuf(f"ktb{n}", [128, 16384], BF16) for n in range(2)]
        tbring = Ring([P.sbuf(f"tb{i}", [128, 2048], F32) for i in range(2)])
        wring = Ring([P.sbuf(f"wn{i}", [128, 512], F32) for i in range(8)])
        kfring = Ring([P.sbuf(f"kx{i}", [128, 512], F32) for i in range(8)])
        nrm = P.sbuf("nrm", [128, 2, 40], F32)
        for cc in range(4):
            for bg in range(8):
                blks = [bg * 4 + i for i in range(4)]
                dr = blks[0] // 16
                tb = tbring.next()
                P.dma("sp", tb[:], text_d[:, bg * 2048:(bg + 1) * 2048].partition_broadcast(128).rearrange("p o d -> p (o d)"), reads=[text_d], writes=[tb])
                wns = []
                for i in range(4):
                    wn = wring.next()
                    wns.append(wn)
                    P.op("act", lambda e, wn=wn, tb=tb, cc=cc, i=i: e.activation(out=wn[:], in_=tb[:, i * 512:(i + 1) * 512], func=AF.Exp, scale=negd[:, cc:cc + 1]),
                         reads=[tb, negd], writes=[wn])
                for n in range(2):
                    col0 = n * 1024 + dr * 512 + cc * 128
                    bks, kxs = [], []
                    for i, blk in enumerate(blks):
                        bk = allb.next()
                        bks.append(bk)
                        sl = slice(blk * 512, (blk + 1) * 512)
                        mm3(bk[:, 0:512], "w4", col0, col0 + 128, h3h[:, sl], h3l[:, sl], [h3h, h3l], bk)
                    for i in range(4):
                        kx = kfring.next()
                        kxs.append(kx)
                        P.op("dve", lambda e, kx=kx, bk=bks[i], wn=wns[i]: e.tensor_tensor(out=kx[:], in0=bk[:, 0:512], in1=wn[:], op=ALU.mult), reads=[bks[i], wns[i]], writes=[kx])
                    for i, blk in enumerate(blks):
                        P.op("dve", lambda e, kx=kxs[i], n=n, blk=blk: e.tensor_reduce(out=nrm[:, n, blk:blk + 1], in_=kx[:], axis=AX.X, op=ALU.add, apply_absolute_value=True),
                             reads=[kxs[i]], djw=[nrm])
                    for i, blk in enumerate(blks):
                        P.op("act", lambda e, kx=kxs[i], n=n, blk=blk: e.activation(out=ktb[n][:, blk * 512:(blk + 1) * 512], in_=kx[:], func=AF.Copy), reads=[kxs[i]], djw=[ktb[n]])
            for n in range(2):
                bk = allb.next()
                colb = n * 1024 + 512 + cc * 128
                mm3(bk[:, 0:1], "w4", colb, colb + 128, h3h[:, 0:1], h3l[:, 0:1], [h3h, h3l], bk)
                P.op("act", lambda e, bk=bk, n=n: e.activation(out=nrm[:, n, 32:33], in_=bk[:, 0:1], func=AF.Abs), reads=[bk], djw=[nrm])
                P.op("dve", lambda e, n=n: e.tensor_reduce(out=nrm[:, n, 34:35], in_=nrm[:, n, 0:16], axis=AX.X, op=ALU.add), reads=[nrm], djw=[nrm])
                P.op("dve", lambda e, n=n: e.tensor_reduce(out=nrm[:, n, 35:36], in_=nrm[:, n, 16:33], axis=AX.X, op=ALU.add), reads=[nrm], djw=[nrm])
                P.op("dve", lambda e, n=n: e.reciprocal(out=nrm[:, n, 36:38], in_=nrm[:, n, 34:36]), reads=[nrm], djw=[nrm])
                P.op("dve", lambda e, n=n: e.tensor_scalar(out=ktb[n][:, 0:8192], in0=ktb[n][:, 0:8192], scalar1=nrm[:, n, 36:37], scalar2=None, op0=ALU.mult),
                     reads=[ktb[n], nrm], djw=[ktb[n]])
                P.op("act", lambda e, n=n: e.activation(out=ktb[n][:, 8192:16384], in_=ktb[n][:, 8192:16384], func=AF.Copy, scale=nrm[:, n, 37:38]),
                     reads=[ktb[n], nrm], djw=[ktb[n]])
                P.dma("pool", KTM[n][cc * 128:(cc + 1) * 128, :], ktb[n][:], reads=[ktb[n]], writes=[], djw=[KTM])
        P.pop_scope()

        F1d = P.sbuf("F1d_s", [64, 256], BF16)
        F1k = P.sbuf("F1k_s", [128, 256], BF16)
        F3 = P.sbuf("F3_s", [128, 2, 256], BF16)
        for t_, d_ in ((F1d, F1d_d), (F1k, F1k_d), (F3, F3_d)):
            P.dma("sp", t_[:], d_[:], reads=[d_], writes=[t_])
        src = P.sbuf("fsrc", [128, 128, 128], BF16)
        Ybuf = P.sbuf("Ybuf", [128, 128, 2, 128], BF16)
        Pbuf = P.sbuf("Pbuf", [128, 2, 128, 128], BF16)
        gtring = Ring([P.sbuf(f"gt{i}", [128, 4, 384], BF16) for i in range(2)])
        giring = Ring([P.sbuf(f"gi{i}", [128, 4, 128], BF16) for i in range(2)])
        xrring = Ring([P.sbuf(f"xq{i}", [128, 2, 512], BF16) for i in range(2)])
        kfr = Ring([P.sbuf(f"kfs{i}", [128, 2, 512], BF16) for i in range(3)])
        tring = Ring([P.sbuf(f"tt{i}", [128, 512], F32) for i in range(6)])
        evi = [0]

        def evac(out_ap, in_ap, reads, writes, djw=()):
            eng = "act" if evi[0] % 2 == 0 else "dve"
            evi[0] += 1
            if eng == "act":
                P.op("act", lambda e: e.activation(out=out_ap, in_=in_ap, func=AF.Copy), reads=reads, writes=writes, djw=djw)
            else:
                P.op("dve", lambda e: e.tensor_copy(out=out_ap, in_=in_ap), reads=reads, writes=writes, djw=djw)

        NKG = 17
        KH = 68

        sguard = []

        def fft_fwd(A, F1, consumer):
            for c2 in range(64):
                bk = allb.next()
                for u in range(2):
                    c = 2 * c2 + u
                    P.op("pe", lambda e, bk=bk, u=u, c=c: e.matmul(bk[:, u * 256:(u + 1) * 256], lhsT=src[0:A, c, :], rhs=F1[0:A, :], start=True, stop=True),
                         reads=[src, F1] + sguard, writes=[bk], pe_acc=True)
                evac(Ybuf[:, 2 * c2:2 * c2 + 2, :, :].rearrange("p c r k -> p (c r k)"), bk[:, 0:512], [bk], [], djw=[Ybuf])
            for kg in range(NKG):
                gt = gtring.next()
                P.dma("sp", gt[:], GT_d[kg * 4:(kg + 1) * 4].rearrange("k b x -> b k x"), reads=[GT_d], writes=[gt])
                br = allb.next()
                bi = allb.next()
                for u in range(4):
                    k1 = kg * 4 + u
                    sl = slice(u * 128, (u + 1) * 128)
                    P.op("pe", lambda e, br=br, gt=gt, u=u, k1=k1, sl=sl: e.matmul(br[:, sl], lhsT=gt[:, u, 0:128], rhs=Ybuf[:, :, 0, k1], start=True, stop=False), reads=[gt, Ybuf], writes=[br], pe_acc=True)
                    P.op("pe", lambda e, br=br, gt=gt, u=u, k1=k1, sl=sl: e.matmul(br[:, sl], lhsT=gt[:, u, 256:384], rhs=Ybuf[:, :, 1, k1], start=False, stop=True), reads=[gt, Ybuf], writes=[br], pe_acc=True)
                    P.op("pe", lambda e, bi=bi, gt=gt, u=u, k1=k1, sl=sl: e.matmul(bi[:, sl], lhsT=gt[:, u, 128:256], rhs=Ybuf[:, :, 0, k1], start=True, stop=False), reads=[gt, Ybuf], writes=[bi], pe_acc=True)
                    P.op("pe", lambda e, bi=bi, gt=gt, u=u, k1=k1, sl=sl: e.matmul(bi[:, sl], lhsT=gt[:, u, 0:128], rhs=Ybuf[:, :, 1, k1], start=False, stop=True), reads=[gt, Ybuf], writes=[bi], pe_acc=True)
                consumer(kg, br, bi)

        def fft_inv(A):
            for c2 in range(64):
                bk = allb.next()
                for u in range(2):
                    c = 2 * c2 + u
                    P.op("pe", lambda e, bk=bk, u=u, c=c: e.matmul(bk[0:KH, u * 256:(u + 1) * 256], lhsT=Pbuf[:, 0, 0:KH, c], rhs=F3[:, 0, :], start=True, stop=False), reads=[Pbuf, F3], writes=[bk], pe_acc=True)
                    P.op("pe", lambda e, bk=bk, u=u, c=c: e.matmul(bk[0:KH, u * 256:(u + 1) * 256], lhsT=Pbuf[:, 1, 0:KH, c], rhs=F3[:, 1, :], start=False, stop=True), reads=[Pbuf, F3], writes=[bk], pe_acc=True)
                evac(Ybuf[0:KH, 2 * c2:2 * c2 + 2, :, :].rearrange("p c r k -> p (c r k)"), bk[0:KH, 0:512], [bk], [], djw=[Ybuf])
            for bg in range(32):
                gi = giring.next()
                P.dma("sp", gi[:], GI_d[bg * 4:(bg + 1) * 4].rearrange("b k x -> k b x"), reads=[GI_d], writes=[gi])
                bk = allb.next()
                for u in range(4):
                    b_ = bg * 4 + u
                    sl = slice(u * 128, (u + 1) * 128)
                    P.op("pe", lambda e, bk=bk, gi=gi, u=u, b_=b_, sl=sl: e.matmul(bk[0:A, sl], lhsT=gi[0:KH, u, 0:A], rhs=Ybuf[0:KH, :, 0, b_], start=True, stop=False), reads=[gi, Ybuf], writes=[bk], pe_acc=True)
                    P.op("pe", lambda e, bk=bk, gi=gi, u=u, b_=b_, sl=sl: e.matmul(bk[0:A, sl], lhsT=gi[0:KH, u, 64:64 + A], rhs=Ybuf[0:KH, :, 1, b_], start=False, stop=True), reads=[gi, Ybuf], writes=[bk], pe_acc=True)
                evac(src[0:A, :, bg * 4:(bg + 1) * 4].rearrange("p c b -> p b c"), bk[0:A, 0:512].rearrange("p (b c) -> p b c", b=4), [bk], [], djw=[src])

        for n in range(2):
            for cc in range(4):
                P.dma("sp", src[:], KTM[n][cc * 128:(cc + 1) * 128, :].rearrange("c (a b) -> a c b", b=128), reads=[KTM], writes=[src])

                def store_kf(kg, br, bi, n=n, cc=cc):
                    xs = xrring.next()
                    evac(xs[:, 0, :], br[:, 0:512], [br], [xs])
                    evac(xs[:, 1, :], bi[:, 0:512], [bi], [], djw=[xs])
                    P.dma("pool", KSP[n][cc][kg], xs[:], reads=[xs], writes=[], djw=[KSP])
                fft_fwd(128, F1k, store_kf)

        CG = 16
        hbb = P.sbuf("hbb", [64, 2, 512], F32)
        P.dma("sp", hbb[:].rearrange("p n c -> p (n c)"), hbrow_d.h.partition_broadcast(64).rearrange("p o d -> p (o d)"), reads=[hbrow_d], writes=[hbb])
        pflat = Pbuf[:].rearrange("p r k c -> p (r k c)").bitcast(F32)
        galias = [Tl(pflat[0:64, i * CG * 128:(i + 1) * CG * 128].rearrange("p (c b) -> p c b", c=CG), f"ga{i}") for i in range(8)]
        srcg = Tl(None, "srcg")
        gxr = Ring(galias[0:4])
        gzr = Ring(galias[4:8])
        for cc in range(4):
            rows = slice(cc * 128, (cc + 1) * 128)
            P.dma("pool", src[0:64, :, :], UC[rows, 1:L + 1].rearrange("c (a b) -> a c b", b=128), reads=[UC], writes=[src, srcg])
            if not sguard:
                sguard.append(srcg)
            for n in range(2):
                def mult(kg, br, bi, n=n, cc=cc):
                    ks = kfr.next()
                    P.dma("sp", ks[:], KSP[n][cc][kg], reads=[KSP], writes=[ks])
                    t1, t2, t3, t4 = (tring.next() for _ in range(4))
                    P.op("dve", lambda e: e.tensor_tensor(out=t1[:], in0=br[:, 0:512], in1=ks[:, 0, :], op=ALU.mult), reads=[br, ks], writes=[t1])
                    P.op("dve", lambda e: e.tensor_tensor(out=t2[:], in0=bi[:, 0:512], in1=ks[:, 1, :], op=ALU.mult), reads=[bi, ks], writes=[t2])
                    P.op("dve", lambda e: e.tensor_tensor(out=t3[:], in0=br[:, 0:512], in1=ks[:, 1, :], op=ALU.mult), reads=[br, ks], writes=[t3])
                    P.op("dve", lambda e: e.tensor_tensor(out=t4[:], in0=bi[:, 0:512], in1=ks[:, 0, :], op=ALU.mult), reads=[bi, ks], writes=[t4])
                    P.op("pool", lambda e: e.tensor_tensor(out=Pbuf[:, 0, kg * 4:(kg + 1) * 4, :].rearrange("p k c -> p (k c)"), in0=t1[:], in1=t2[:], op=ALU.subtract),
                         reads=[t1, t2], djw=[Pbuf])
                    P.op("pool", lambda e: e.tensor_tensor(out=Pbuf[:, 1, kg * 4:(kg + 1) * 4, :].rearrange("p k c -> p (k c)"), in0=t3[:], in1=t4[:], op=ALU.add),
                         reads=[t3, t4], djw=[Pbuf])
                fft_fwd(64, F1d, mult)
                fft_inv(64)
                for g in range(128 // CG):
                    cs = slice(g * CG, (g + 1) * CG)
                    r0 = cc * 128 + g * CG
                    gx, gz = gxr.next(), gzr.next()
                    gd = [Pbuf] if g < 4 else []
                    P.dma("sp", gx[:], UC[(1 + n) * 512 + r0:(1 + n) * 512 + r0 + CG, 1:L + 1].rearrange("c (a b) -> a c b", b=128), reads=[UC], writes=[gx], djw=gd)
                    if n == 0:
                        P.dma("sp", gz[:], UC[r0:r0 + CG, 1:L + 1].rearrange("c (a b) -> a c b", b=128), reads=[UC], writes=[gz], djw=gd)
                    else:
                        P.dma("sp", gz[:], Z1F[r0:r0 + CG, :].rearrange("c (a b) -> a c b", b=128), reads=[Z1F], writes=[gz], djw=gd)
                    P.op("dve", lambda e, gz=gz, n=n, r0=r0: e.tensor_tensor(out=gz[:], in0=gz[:], in1=hbb[:, n, r0:r0 + CG].unsqueeze(2).to_broadcast([64, CG, 128]), op=ALU.mult),
                         reads=[gz, hbb, Pbuf], writes=[gz])
                    P.op("dve", lambda e, gz=gz, cs=cs: e.tensor_tensor(out=gz[:], in0=gz[:], in1=src[0:64, cs, :], op=ALU.add), reads=[gz, src, Pbuf], writes=[gz])
                    P.op("dve", lambda e, gz=gz, gx=gx: e.tensor_tensor(out=gz[:], in0=gz[:], in1=gx[:], op=ALU.mult), reads=[gz, gx, Pbuf], writes=[gz])
                    P.op("act", lambda e, gz=gz, cs=cs: e.activation(out=src[0:64, cs, :], in_=gz[:], func=AF.Copy), reads=[gz, Pbuf], djw=[srcg])
                    if n == 0:
                        P.dma("pool", Z1F[r0:r0 + CG, :].rearrange("c (a b) -> a c b", b=128), gz[:], reads=[gz, Pbuf], writes=[], djw=[Z1F])
                if n == 1:
                    P.dma("pool", YT[rows, :].rearrange("c (a b) -> a c b", b=128), src[0:64, :, :], reads=[src, srcg], writes=[], djw=[YT])
        P.pop_scope()

    if "C" in phases:
        P.push_scope()
        NC_ = L // 128
        lgf = P.sbuf("lgf", [128, 4], F32)
        lgb = P.sbuf("lgb", [128, 4], F32)
        P.dma("sp", lgf[:], lgf_d.h.partition_broadcast(128).rearrange("p o d -> p (o d)"), reads=[lgf_d], writes=[lgf])
        P.dma("sp", lgb[:], lgb_d.h.partition_broadcast(128).rearrange("p o d -> p (o d)"), reads=[lgb_d], writes=[lgb])
        rtab = P.sbuf("rtab_s", [128, 4, 128], F32)
        itab = P.sbuf("itab_s", [128, 2, 128], F32)
        jtab = P.sbuf("jtab_s", [128, 3], F32)
        P.dma("sp", rtab[:], rtab_d[:], reads=[rtab_d], writes=[rtab])
        P.dma("sp", itab[:], itab_d[:], reads=[itab_d], writes=[itab])
        P.dma("sp", jtab[:], jtab_d[:], reads=[jtab_d], writes=[jtab])
        flagc = P.sbuf("flagc", [128, 1], F32)
        P.dma("sp", flagc[:], flag_d[:], reads=[flag_d], writes=[flagc])
        omf = P.sbuf("omf", [128, 1], F32)
        P.op("dve", lambda e: e.tensor_scalar(out=omf[:], in0=flagc[:], scalar1=-1.0, scalar2=1.0, op0=ALU.mult, op1=ALU.add),
             reads=[flagc], writes=[omf])
        DmT = P.sbuf("DmT", [128, 4, 128], F32)
        etmp = P.sbuf("etmp", [128, 2, 128], F32)
        wq = P.sbuf("wq", [128, 4, 2, 128], F32)
        wk = P.sbuf("wk", [128, 4, 3], F32)
        gC = P.sbuf("gC", [128, 4, 2], F32)
        wkb = P.sbuf("wkb", [128, 4], F32)
        for h in range(4):
            P.op("act", lambda e, h=h: e.activation(out=etmp[:, 0, :], in_=rtab[:, 0, :], func=AF.Exp, scale=lgf[:, h:h + 1]), reads=[rtab, lgf], writes=[etmp])
            P.op("act", lambda e, h=h: e.activation(out=etmp[:, 1, :], in_=rtab[:, 1, :], func=AF.Exp, scale=lgb[:, h:h + 1]), reads=[rtab, lgb], writes=[etmp])
            P.op("dve", lambda e, h=h: e.tensor_tensor(out=etmp[:], in0=etmp[:], in1=rtab[:, 2:4, :], op=ALU.mult), reads=[etmp, rtab], writes=[etmp])
            P.op("dve", lambda e, h=h: e.tensor_tensor(out=DmT[:, h, :], in0=etmp[:, 0, :], in1=etmp[:, 1, :], op=ALU.add), reads=[etmp], writes=[DmT])
            P.op("act", lambda e, h=h: e.activation(out=wq[:, h, 0, :], in_=itab[:, 0, :], func=AF.Exp, scale=lgf[:, h:h + 1]), reads=[itab, lgf], writes=[wq])
            P.op("act", lambda e, h=h: e.activation(out=wq[:, h, 1, :], in_=itab[:, 1, :], func=AF.Exp, scale=lgb[:, h:h + 1]), reads=[itab, lgb], writes=[wq])
            P.op("act", lambda e, h=h: e.activation(out=wk[:, h, 0:1], in_=jtab[:, 0:1], func=AF.Exp, scale=lgf[:, h:h + 1]), reads=[jtab, lgf], writes=[wk])
            P.op("act", lambda e, h=h: e.activation(out=wkb[:, h:h + 1], in_=jtab[:, 1:2], func=AF.Exp, scale=lgb[:, h:h + 1]), reads=[jtab, lgb], writes=[wkb])
            P.op("act", lambda e, h=h: e.activation(out=gC[:, h, 0:1], in_=jtab[:, 2:3], func=AF.Exp, scale=lgf[:, h:h + 1]), reads=[jtab, lgf], writes=[gC])
            P.op("act", lambda e, h=h: e.activation(out=gC[:, h, 1:2], in_=jtab[:, 2:3], func=AF.Exp, scale=lgb[:, h:h + 1]), reads=[jtab, lgb], writes=[gC])
        kwb1 = P.sbuf("kwb", [128, NC_, 128], BF16)
        hd = [dict(qT=P.sbuf(f"qTs{i}", [128, NC_, 128], BF16), kT=P.sbuf(f"kTs{i}", [128, NC_, 128], BF16),
                   kwb=kwb1, vm=P.sbuf(f"vms{i}", [128, NC_, 128], BF16)) for i in range(2)]
        kwf = P.sbuf("kwf", [128, NC_, 128], BF16)
        RS = [P.sbuf("RSf", [128, NC_, 128], BF16), P.sbuf("RSb", [128, NC_, 128], BF16)]
        Rst = [[P.sbuf(f"R{d}{i}", [128, 128], F32) for i in range(2)] for d in range(2)]
        gmring = Ring([P.sbuf(f"gm{i}", [128, 4, 128], F32) for i in range(2)])
        qfring = Ring([P.sbuf(f"qf{i}", [128, 2, 4, 128], BF16) for i in range(2)])
        PTring = Ring([P.sbuf(f"PT{i}", [128, 4, 128], BF16) for i in range(3)])
        osring = Ring([P.sbuf(f"os{i}", [128, 4, 128], F32) for i in range(2)])
        sqring = Ring([P.sbuf(f"sq{i}", [128, 4, 128], F32) for i in range(2)])
        ybring = Ring([P.sbuf(f"yb{i}", [128, 4, 128], BF16) for i in range(3)])
        yTring = Ring([P.sbuf(f"yT{i}", [128, 512], BF16) for i in range(3)])
        g4ring = Ring([P.sbuf(f"g4{i}", [128, 8], F32) for i in range(4)])
        allb = Ring(banks)

        def load_head(h):
            d_ = hd[h % 2]
            P.dma("sp", d_["qT"][:].rearrange("p n t -> p (n t)"), QT[h], reads=[QT], writes=[d_["qT"]])
            P.dma("sp", d_["kT"][:].rearrange("p n t -> p (n t)"), KTs[h], reads=[KTs], writes=[d_["kT"]])
            P.dma("sp", d_["vm"][:], VM[h], reads=[VM], writes=[d_["vm"]])

        def load_km(h):
            P.dma("sp", kwb1[:], KM[h], reads=[KM], writes=[kwb1])

        load_head(0)
        load_km(0)
        for h in range(4):
            qT, kT, kwb, vm = (hd[h % 2][k_] for k_ in ("qT", "kT", "kwb", "vm"))
            if h + 1 < 4:
                load_head(h + 1)
            P.op("act", lambda e, h=h, kwb=kwb: e.activation(out=kwf[:], in_=kwb[:], func=AF.Copy, scale=wk[:, h, 0:1]), reads=[kwb, wk], writes=[kwf])
            P.op("dve", lambda e, h=h, kwb=kwb: e.tensor_scalar(out=kwb[:], in0=kwb[:], scalar1=wkb[:, h:h + 1], scalar2=None, op0=ALU.mult), reads=[kwb, wkb], writes=[kwb])
            P.op("pool", lambda e: e.memset(RS[0][:, 0, :], 0.0), writes=[RS[0]])
            P.op("pool", lambda e: e.memset(RS[1][:, NC_ - 1, :], 0.0), writes=[RS[1]])
            P.op("pool", lambda e: e.memset(Rst[0][0][:], 0.0), writes=[Rst[0][0]])
            P.op("pool", lambda e: e.memset(Rst[1][0][:], 0.0), writes=[Rst[1][0]])
            cur = [0, 0]
            for j in range(NC_ // 4):
                for d in range(2):
                    bk = allb.next()
                    kw = kwf if d == 0 else kwb
                    ns = [4 * j + c for c in range(4)] if d == 0 else [NC_ - 1 - (4 * j + c) for c in range(4)]
                    for c, n in enumerate(ns):
                        P.op("pe", lambda e, bk=bk, c=c, n=n, kw=kw, vm=vm: e.matmul(bk[:, c * 128:(c + 1) * 128], lhsT=kw[:, n, :], rhs=vm[:, n, :], start=True, stop=True),
                             reads=[kw, vm], writes=[bk], pe_acc=True)
                    for c, n in enumerate(ns):
                        tgt = n + 1 if d == 0 else n - 1
                        if tgt < 0 or tgt > NC_ - 1:
                            continue
                        ra, rb = Rst[d][cur[d] % 2], Rst[d][(cur[d] + 1) % 2]
                        cur[d] += 1
                        P.op("dve", lambda e, ra=ra, rb=rb, bk=bk, c=c, d=d, h=h: e.scalar_tensor_tensor(out=rb[:], in0=ra[:], scalar=gC[:, h, d:d + 1], in1=bk[:, c * 128:(c + 1) * 128],
                                                                                                   op0=ALU.mult, op1=ALU.add), reads=[ra, gC, bk], writes=[rb])
                        if (d == 0 and tgt == NC_ // 2) or (d == 1 and tgt == NC_ // 2 - 1):
                            P.op("dve", lambda e, rb=rb: e.tensor_scalar(out=rb[:], in0=rb[:], scalar1=omf[:, 0:1], scalar2=None, op0=ALU.mult), reads=[rb, omf], writes=[rb])
                        P.op("act", lambda e, rb=rb, d=d, tgt=tgt: e.activation(out=RS[d][:, tgt, :], in_=rb[:], func=AF.Copy), reads=[rb], djw=[RS[d]])

            if h + 1 < 4:
                load_km(h + 1)

            def stA(j, h=h, qT=qT, kT=kT):
                gm = gmring.next()
                P.dma("sp", gm[:], GM[j * 512:(j + 1) * 512, h * 128:(h + 1) * 128].rearrange("(n p) e -> p n e", p=128), reads=[GM], writes=[gm])
                qfb = qfring.next()
                P.op("dve", lambda e: e.tensor_tensor(out=qfb[:, 0, :, :], in0=qT[:, 4 * j:4 * j + 4, :], in1=wq[:, h, 0:1, :].to_broadcast([128, 4, 128]), op=ALU.mult),
                     reads=[qT, wq], writes=[qfb])
                P.op("dve", lambda e: e.tensor_tensor(out=qfb[:, 1, :, :], in0=qT[:, 4 * j:4 * j + 4, :], in1=wq[:, h, 1:2, :].to_broadcast([128, 4, 128]), op=ALU.mult),
                     reads=[qT, wq], djw=[qfb])
                pS = allb.next()
                for c in range(4):
                    n = 4 * j + c
                    P.op("pe", lambda e, c=c, n=n: e.matmul(pS[:, c * 128:(c + 1) * 128], lhsT=kT[:, n, :], rhs=qT[:, n, :], start=True, stop=True),
                         reads=[kT, qT], writes=[pS], pe_acc=True)
                PT = PTring.next()
                P.op("dve", lambda e: e.tensor_tensor(out=PT[:], in0=pS[:, 0:512].rearrange("p (c t) -> p c t", c=4),
                                                      in1=DmT[:, h:h + 1, :].to_broadcast([128, 4, 128]), op=ALU.mult), reads=[pS, DmT], writes=[PT])
                return gm, qfb, PT

            def stB(j, st, vm=vm):
                gm, qfb, PT = st
                po = allb.next()
                for c in range(4):
                    n = 4 * j + c
                    P.op("pe", lambda e, c=c, n=n: e.matmul(po[:, c * 128:(c + 1) * 128], lhsT=PT[:, c, :], rhs=vm[:, n, :], start=True, stop=False),
                         reads=[PT, vm], writes=[po], pe_acc=True)
                    P.op("pe", lambda e, c=c, n=n: e.matmul(po[:, c * 128:(c + 1) * 128], lhsT=qfb[:, 0, c, :], rhs=RS[0][:, n, :], start=False, stop=False),
                         reads=[qfb, RS[0]], writes=[po], pe_acc=True)
                    P.op("pe", lambda e, c=c, n=n: e.matmul(po[:, c * 128:(c + 1) * 128], lhsT=qfb[:, 1, c, :], rhs=RS[1][:, n, :], start=False, stop=True),
                         reads=[qfb, RS[1]], writes=[po], pe_acc=True)
                osb = osring.next()
                P.op("act", lambda e: e.activation(out=osb[:].rearrange("p c e -> p (c e)"), in_=po[:, 0:512], func=AF.Copy), reads=[po], writes=[osb])
                sq = sqring.next()
                P.op("act", lambda e: e.activation(out=sq[:].rearrange("p c e -> p (c e)"), in_=osb[:].rearrange("p c e -> p (c e)"), func=AF.Square), reads=[osb], writes=[sq])
                g4 = g4ring.next()
                P.op("dve", lambda e: e.tensor_reduce(out=g4[:, 0:4], in_=sq[:], axis=AX.X, op=ALU.add), reads=[sq], writes=[g4])
                P.op("act", lambda e: e.activation(out=g4[:, 4:8], in_=g4[:, 0:4], func=AF.Sqrt, bias=EPS, scale=1.0 / 128), reads=[g4], writes=[g4])
                P.op("dve", lambda e: e.reciprocal(out=g4[:, 4:8], in_=g4[:, 4:8]), reads=[g4], writes=[g4])
                P.op("dve", lambda e: e.tensor_tensor(out=sq[:], in0=osb[:], in1=g4[:, 4:8].unsqueeze(2).to_broadcast([128, 4, 128]), op=ALU.mult),
                     reads=[osb, g4], writes=[sq])
                yb = ybring.next()
                P.op("pool", lambda e: e.tensor_tensor(out=yb[:], in0=sq[:], in1=gm[:], op=ALU.mult), reads=[sq, gm], writes=[yb])
                return yb

            def stC(j, yb, h=h):
                bt = allb.next()
                btv = bt[:].bitcast(BF16)
                for c in range(4):
                    P.op("pe", lambda e, c=c: e.transpose(out=btv[:, c * 128:(c + 1) * 128], in_=yb[:, c, :], identity=ident[:]),
                         reads=[yb, ident], writes=[bt], pe_acc=True)
                yT = yTring.next()
                P.op("act", lambda e: e.activation(out=yT[:], in_=btv[:, 0:512], func=AF.Copy), reads=[bt], writes=[yT])
                P.dma("pool", YT[512 + h * 128:512 + (h + 1) * 128, j * 512:(j + 1) * 512], yT[:], reads=[yT], writes=[], djw=[YT])

            NG = NC_ // 4
            sA = {0: stA(0)}
            sB = {}
            for j in range(NG + 1):
                if j + 1 < NG:
                    sA[j + 1] = stA(j + 1)
                if j < NG:
                    sB[j] = stB(j, sA.pop(j))
                if j >= 1:
                    stC(j - 1, sB.pop(j - 1))
        P.pop_scope()

    if "D" in phases:
        P.push_scope()
        alloc_ffn()
        if "A" not in phases:
            cast_weights(1)
        gpm = load_gpost("mix_post_g", False)
        gp2 = load_gpost("ffn2_post_g", True)
        load_w2(1)
        wouts = P.sbuf("wouts", [128, 8, D], BF16)
        for k in range(8):
            P.dma("pool", wouts[:, k, :], wout_d[:, k * D:(k + 1) * D], reads=[wout_d], writes=[wouts])
        ymring = Ring([P.sbuf(f"ym{i}", [128, 8, T], BF16) for i in range(2)])
        ycnt = [0]

        def s0(b):
            ym = ymring.next()
            P.dma("sp", ym[:], YT[:, b * T:(b + 1) * T].rearrange("(k p) t -> p k t", p=128), reads=[YT], writes=[ym])
            st = string.next()
            hns = []
            for tt in range(TT):
                ys = ysets[ycnt[0] % 2]
                ycnt[0] += 1
                mm_tokmajor(ys, [ym[:, k, tt * 128:(tt + 1) * 128] for k in range(8)],
                            lambda i, dh: wouts[:, i, dh * 512:(dh + 1) * 512], lambda i: [ym, wouts])
                xt = xring.next()
                r0 = b * T + tt * 128
                P.dma("sp", xt[:], X1[r0:r0 + 128, :], reads=[X1], writes=[xt])
                xo = xoring.next()
                epilogue(ys, xt, gpm, xo)
                P.dma("pool", X2[r0:r0 + 128, :], xo[:], reads=[xo], writes=[], djw=[X2])
                prep_tile(xo, st, tt, None)
                rstd_cols(st, tt, 1)
                hns.append(norm_cast(xo, st, tt))
            return hns

        hT_cur = prep_pe(s0(0), "ffn2_pre_g")
        for b in range(nb):
            ffn_s3(hT_cur, w13b[1])
            if b + 1 < nb:
                hns_next = s0(b + 1)
            for tt in range(TT):
                if tt == TT - 1 and b + 1 < nb:
                    hT_cur = prep_pe(hns_next, "ffn2_pre_g")
                ys = ysets[ycnt[0] % 2]
                ycnt[0] += 1
                mm_tokmajor(ys, [aT[fc][:, tt * 128:(tt + 1) * 128] for fc in range(KF)],
                            lambda i, dh: w2s[:, i, dh * 512:(dh + 1) * 512], lambda i: [aT[i], w2s])
                xt = xring.next()
                r0 = b * T + tt * 128
                P.dma("sp", xt[:], X2[r0:r0 + 128, :], reads=[X2], writes=[xt])
                xo = xoring.next()
                epilogue(ys, xt, gp2, xo)
                P.dma("pool", out_d[r0:r0 + 128, :], xo[:], reads=[xo], writes=[], djw=[out_d])
        P.pop_scope()

    P.finish_all("sp")
    P.emit()
    return nc, P


def _shared_maps(inp):
    f = np.float32
    m = {}
    for i, nm in ((1, "ffn1"), (2, "ffn2")):
        w1 = np.asarray(inp[nm + "_w1"][0], f)
        w3 = np.asarray(inp[nm + "_w3"][0], f)
        w2 = np.asarray(inp[nm + "_w2"][0], f)
        a = np.stack([w1, w3], 0).reshape(2, KD, 128, KF, 128)
        m[f"w13_{i}"] = np.ascontiguousarray(a.transpose(3, 2, 0, 1, 4)).reshape(KF, 128, 2 * KD * 128)
        m[f"w2_{i}"] = np.ascontiguousarray(w2.reshape(KF, 128, D).transpose(1, 0, 2)).reshape(128, KF * D)
    win = np.asarray(inp["w_in"][0], f)
    cols = [win[:, c * 128:(c + 1) * 128] for c in range(NHC)]
    cols += [win[:, 1536 + h * 128:1536 + (h + 1) * 128] for h in range(4)]
    cols += [win[:, 2048 + h * 128:2048 + (h + 1) * 128] for h in range(4)]
    fm = np.stack(cols, 0).reshape(20, KD, 128, 128).transpose(0, 2, 1, 3)
    m["win_fm"] = np.ascontiguousarray(fm).reshape(20, 128, KD * 128)
    tm = np.stack([win[:, 2560:3072], win[:, 3072:3584]], 0).reshape(2, KD, 128, 512).transpose(0, 2, 1, 3)
    m["win_tm"] = np.ascontiguousarray(tm).reshape(2, 128, KD * 512)
    m["wout"] = np.ascontiguousarray(np.asarray(inp["w_out"][0], f).reshape(8, 128, D).transpose(1, 0, 2)).reshape(128, 8 * D)
    for k in ("ffn1_pre_g", "mix_pre_g", "ffn2_pre_g"):
        m[k] = np.ascontiguousarray(np.asarray(inp[k][0], f).reshape(KD, 128).T)
    for k in ("ffn1_post_g", "mix_post_g", "ffn2_post_g"):
        m[k] = np.asarray(inp[k][0], f).reshape(1, D)
    sw = np.asarray(inp["short_w"][0], f)
    m["short_w"] = np.ascontiguousarray(sw.reshape(3, NHC, 128).transpose(2, 1, 0))
    m["short_b"] = np.ascontiguousarray(np.asarray(inp["short_b"][0], f).reshape(NHC, 128).T)
    m["ident"] = np.eye(128, dtype=np.float32).astype(ml_dtypes.bfloat16)
    pm = np.zeros((128, 128), f)
    for i in range(128):
        pm[(i + 64) % 128, i] = 1.0
    m["pswap"] = pm
    dl = np.linspace(math.log(1e-2) / 1.5, math.log(1e-2) / 0.3, 512, dtype=f)
    m["negd"] = np.ascontiguousarray((-np.abs(dl)).reshape(4, 128).T)
    for kk in ("filt_w1", "filt_w2", "filt_w3", "filt_w4"):
        m[kk] = np.ascontiguousarray(np.asarray(inp[kk][0], f))
    m["filt_b"] = np.ascontiguousarray(np.stack([np.asarray(inp[kk][0], f) for kk in ("filt_b1", "filt_b2", "filt_b3")], 1))
    m["filt_freq"] = np.asarray(inp["filt_freq"][0], f).reshape(64, 1)
    m["hbias_row"] = np.ascontiguousarray(np.asarray(inp["hyena_bias"][0], f).reshape(1, 1024))
    m["ret_log_decay_f"] = np.asarray(inp["ret_log_decay_f"], f).reshape(1, 4)
    m["ret_log_decay_b"] = np.asarray(inp["ret_log_decay_b"], f).reshape(1, 4)
    si = np.arange(128)[:, None]
    ti = np.arange(128)[None, :]
    diff = (ti - si).astype(f)
    m["rtab"] = np.ascontiguousarray(np.stack([np.maximum(diff, 0), np.maximum(-diff, 0), (diff >= 0).astype(f), (diff < 0).astype(f)], 1))
    ii = np.arange(128, dtype=f)
    m["itab"] = np.ascontiguousarray(np.broadcast_to(np.stack([ii + 1, 128 - ii], 0)[None], (128, 2, 128)))
    m["jtab"] = np.ascontiguousarray(np.stack([127 - ii, ii, np.full(128, 128.0, f)], 1))
    return m


def _core_tables(seq_len):
    f = np.float32
    m = {}
    pos = (np.arange(L) % seq_len).astype(f)
    inv = (1.0 / (10000.0 ** (np.arange(0, 128, 2, dtype=f) / f(128)))).astype(f)
    ang = (pos[:, None] * inv[None, :]).astype(f)
    c = np.cos(ang).astype(f)
    s = np.sin(ang).astype(f)
    cosT = np.concatenate([c, c], 1).T
    sinT = np.concatenate([-s, s], 1).T
    rot = np.stack([cosT, sinT], 1).reshape(128, 2, NB, T).transpose(2, 0, 1, 3)
    m["rot"] = np.ascontiguousarray(rot, dtype=f)
    m["pflag"] = np.full((128, 1), 0.0 if seq_len == L else 1.0, f)
    lag = np.arange(16384)
    mm = np.where(lag < 8192, lag, 16384 - lag)
    valid = np.where(lag < 8192, mm < seq_len, (mm >= 1) & (mm <= seq_len - 1))
    mi = np.where(valid, mm, 0)
    tl = np.linspace(0.0, 1.0, seq_len, dtype=f)
    wl = (f(2.0 * math.pi) * np.arange(seq_len, dtype=f) / f(seq_len)).astype(f)
    fbv = np.linspace(1e-4, 15, 16, dtype=f)
    zfull = np.concatenate([tl[:, None], np.cos(fbv[None, :] * wl[:, None]), -np.sin(fbv[None, :] * wl[:, None])], 1).astype(f)
    zt_ = np.ascontiguousarray(zfull[mi].T)
    zhi = zt_.astype(ml_dtypes.bfloat16)
    zlo = (zt_ - zhi.astype(f)).astype(ml_dtypes.bfloat16)
    m["zext"] = np.ascontiguousarray(np.stack([zhi, zlo], 1))
    m["text"] = np.where(valid, tl[mi], f(1e4)).astype(f).reshape(1, 16384)
    amap = np.arange(64) if seq_len == L else np.concatenate([np.arange(32), 64 + np.arange(32)])
    k = np.arange(128)
    bf = ml_dtypes.bfloat16
    th = 2 * np.pi * np.outer(amap, k) / 128.0
    m["F1d"] = np.concatenate([np.cos(th), -np.sin(th)], 1).astype(bf)
    th = 2 * np.pi * np.outer(np.arange(128), k) / 128.0
    m["F1k"] = np.concatenate([np.cos(th), -np.sin(th)], 1).astype(bf)
    m["F3"] = np.ascontiguousarray(np.stack([np.concatenate([np.cos(th), np.sin(th)], 1), np.concatenate([-np.sin(th), np.cos(th)], 1)], 1)).astype(bf)
    b_ = np.arange(128)
    phi = 2 * np.pi * (b_[None, :, None] * k[None, None, :] / 128.0 + b_[None, :, None] * k[:, None, None] / 16384.0)
    m["GT"] = np.ascontiguousarray(np.concatenate([np.cos(phi), -np.sin(phi), np.sin(phi)], 2)).astype(bf)
    psi = 2 * np.pi * (amap[None, None, :] * k[None, :, None] / 128.0 + b_[:, None, None] * k[None, :, None] / 16384.0)
    wk1 = np.where((k == 0) | (k == 64), 1.0, np.where(k < 64, 2.0, 0.0))[None, :, None]
    m["GI"] = np.ascontiguousarray(np.concatenate([np.cos(psi), -np.sin(psi)], 2) * wk1 / 16384.0).astype(bf)
    return m


_CACHE = {}


def kernel(**inputs):
    inp = {k: np.asarray(v) for k, v in inputs.items()}
    xs = np.asarray(inp["x_sample"], np.float32)
    xp = np.asarray(inp["x_prompt"], np.float32)
    shared = _shared_maps(inp)
    tab_s = _core_tables(L)
    tab_p = _core_tables(L // 2)
    in_maps = []
    for c in range(8):
        m = dict(shared)
        if c < 4:
            m.update(tab_s)
            m["x"] = np.ascontiguousarray(xs[c])
        else:
            j = min(c - 4, 1)
            m.update(tab_p)
            m["x"] = np.ascontiguousarray(np.concatenate([xp[2 * j], xp[2 * j + 1]], 0))
        in_maps.append(m)
    if "nc" not in _CACHE:
        _CACHE["nc"] = build()[0]
    res = run_bass_kernel_spmd(_CACHE["nc"], in_maps, core_ids=list(range(8)))
    outs = [np.asarray(r["out"], np.float32) for r in res.results]
    y_sample = np.stack(outs[0:4], 0)
    y_prompt = np.stack([outs[4][:L // 2], outs[4][L // 2:], outs[5][:L // 2], outs[5][L // 2:]], 0)
    return (y_prompt, y_sample)
```

```python
import math
import numpy as np
import ml_dtypes
from contextlib import ExitStack
import concourse.bass as bass
import concourse.mybir as mybir
from concourse.bass_utils import run_bass_kernel_spmd

F32 = mybir.dt.float32
BF16 = mybir.dt.bfloat16
AF = mybir.ActivationFunctionType
ALU = mybir.AluOpType
AX = mybir.AxisListType

EPOCH = 12000
ARENA_BYTES = 212832
ENGS = ("pe", "act", "dve", "pool", "sp")

D = 1024
KD = 8
FF = 2816
KF = 22
T = 512
TT = 4
L = 8192
NB = L // T
NHC = 12
EPS = 1e-6


class Tl:
    __slots__ = ("h", "name", "lw", "rd", "lwx")

    def __init__(self, h, name=""):
        self.h = h
        self.name = name
        self.lw = {}
        self.lwx = None
        self.rd = {}

    def __getitem__(self, k):
        return self.h[k]


class Op:
    __slots__ = ("fn", "inc", "lane", "idx")

    def __init__(self, fn, lane, idx):
        self.fn = fn
        self.lane = lane
        self.idx = idx
        self.inc = False


class Prog:
    def __init__(self, nc, n_sp_lanes=14, n_pool_lanes=8):
        self.nc = nc
        self.es = ExitStack()
        self.streams = {e: [] for e in ENGS}
        self.lanes = {e: [] for e in ENGS}
        self.clock = {}
        self.known = {e: {} for e in ENGS}
        self.dma_lanes = {"sp": [f"sp{i}" for i in range(n_sp_lanes)],
                          "pool": [f"pl{i}" for i in range(n_pool_lanes)]}
        self.dma_rr = {"sp": 0, "pool": 0}
        for q in self.dma_lanes.values():
            for l in q:
                self.lanes[l] = []
        self.nops = 0
        self.nalloc = 0
        self.scopes = []
        self.arena = None
        self.arena_off = 0
        self.peak = []

    def sbuf(self, name, shape, dt):
        if self.arena is None:
            self.arena = self.es.enter_context(self.nc.sbuf_tensor("arena", [128, ARENA_BYTES], mybir.dt.uint8))
        isz = 4 if dt == F32 else 2
        n = 1
        for d_ in shape[1:]:
            n *= d_
        nbytes = (n * isz + 31) // 32 * 32
        off = self.arena_off
        assert off + nbytes <= ARENA_BYTES, f"SBUF arena overflow allocating {name} {shape}: {off}+{nbytes}"
        self.arena_off += nbytes
        ap = self.arena[0:shape[0], off:off + n * isz].bitcast(dt)
        if len(shape) == 3:
            ap = ap.rearrange("p (a b) -> p a b", a=shape[1])
        elif len(shape) == 4:
            ap = ap.rearrange("p (a b c) -> p a b c", a=shape[1], b=shape[2])
        return Tl(ap, name)

    def push_scope(self):
        self.scopes.append(self.arena_off)

    def pop_scope(self):
        self.barrier()
        self.peak.append(self.arena_off)
        self.arena_off = self.scopes.pop()

    def barrier(self):
        deps = [(l, len(ops) - 1) for l, ops in self.lanes.items() if ops]
        for e in ENGS:
            self._wait_for(e, deps)

    def psum(self, name, shape, dt=F32):
        return Tl(self.es.enter_context(self.nc.psum_tensor(name, list(shape), dt)), name)

    def dram(self, name, shape, dt, kind="Internal"):
        return Tl(self.nc.dram_tensor(name, list(shape), dt, kind=kind).ap(), name)

    def _deps(self, reads, writes, pe_acc, djw=()):
        deps = []
        for t in reads:
            deps.extend(t.lw.items())
        for t in writes:
            for l, i in t.lw.items():
                if not (pe_acc and l == "pe"):
                    deps.append((l, i))
            for l, i in t.rd.items():
                if not (pe_acc and l == "pe"):
                    deps.append((l, i))
        for t in djw:
            if t.lwx is not None:
                deps.append(t.lwx)
            deps.extend(t.rd.items())
        return deps

    def _wait_for(self, eng, deps):
        kn = self.known[eng]
        for (l, i) in sorted(set(deps), key=lambda d: -d[1]):
            if kn.get(l, -1) >= i:
                continue
            self.lanes[l][i].inc = True
            self.streams[eng].append(("w", l, i))
            for l2, i2 in self.clock[(l, i)].items():
                if kn.get(l2, -1) < i2:
                    kn[l2] = i2

    def _finish(self, lane, idx, eng, reads, writes, djw=()):
        c = dict(self.known[eng])
        c[lane] = idx
        self.clock[(lane, idx)] = c
        for t in reads:
            t.rd[lane] = idx
        for t in writes:
            t.lw = {lane: idx}
            t.lwx = (lane, idx)
            t.rd = {}
        for t in djw:
            t.lw[lane] = idx

    def op(self, eng, fn, reads=(), writes=(), pe_acc=False, djw=()):
        self._wait_for(eng, self._deps(reads, writes, pe_acc, djw))
        idx = len(self.lanes[eng])
        o = Op(fn, eng, idx)
        self.lanes[eng].append(o)
        self.streams[eng].append(("o", o))
        self._finish(eng, idx, eng, reads, writes, djw)
        self.nops += 1
        return o

    def dma(self, q, out, in_, reads=(), writes=(), djw=(), **kw):
        ls = self.dma_lanes[q]
        lane = ls[self.dma_rr[q] % len(ls)]
        self.dma_rr[q] += 1
        deps = self._deps(reads, writes, False, djw)
        idx = len(self.lanes[lane])
        if idx > 0:
            deps.append((lane, idx - 1))
        self._wait_for(q, deps)
        o = Op(lambda e: e.dma_start(out=out, in_=in_, **kw), lane, idx)
        o.inc = True
        self.lanes[lane].append(o)
        self.streams[q].append(("o", o))
        self._finish(lane, idx, q, reads, writes, djw)
        self.nops += 1
        return o

    def finish_all(self, q="sp"):
        deps = []
        for qq, ls in self.dma_lanes.items():
            for l in ls:
                if self.lanes[l]:
                    deps.append((l, len(self.lanes[l]) - 1))
        self._wait_for(q, deps)

    def emit(self):
        nc = self.nc
        cnt = {}
        sems = {}
        for l, ops in self.lanes.items():
            c = 0
            for o in ops:
                if o.inc:
                    c += 1
                    cnt[(l, o.idx)] = c
            for e in range((c + EPOCH - 1) // EPOCH):
                sems[(l, e)] = self.es.enter_context(nc.semaphore(f"s_{l}_{e}"))
        self.n_sems = len(sems)

        def isdma(l):
            return l not in ENGS

        def run(engobj, items):
            for it in items:
                if it[0] == "w":
                    c = cnt[(it[1], it[2])] - 1
                    v = c % EPOCH + 1
                    engobj.wait_ge(sems[(it[1], c // EPOCH)], v * 16 if isdma(it[1]) else v)
                else:
                    o = it[1]
                    ins = o.fn(engobj)
                    if o.inc:
                        c = cnt[(o.lane, o.idx)] - 1
                        ins.then_inc(sems[(o.lane, c // EPOCH)], 16 if isdma(o.lane) else 1)

        with nc.Block() as block:
            @block.sync
            def _(e):
                run(e, self.streams["sp"])

            @block.tensor
            def _(e):
                run(e, self.streams["pe"])

            @block.scalar
            def _(e):
                run(e, self.streams["act"])

            @block.vector
            def _(e):
                run(e, self.streams["dve"])

            @block.gpsimd
            def _(e):
                run(e, self.streams["pool"])
        self.es.close()


class Ring:
    def __init__(self, tiles):
        self.t = tiles
        self.i = 0

    def next(self):
        t = self.t[self.i % len(self.t)]
        self.i += 1
        return t


def build(phases=("A", "B", "C", "D"), nb=NB, debug_outs=(), ext_in=()):
    nc = bass.Bass("TRN2", target_bir_lowering=False)
    P = Prog(nc)
    dbg = set(debug_outs)

    def din(name, shape, dt=F32):
        return P.dram(name, shape, dt, kind="ExternalInput")

    def dscr(name, shape, dt):
        kind = "ExternalOutput" if name in dbg else ("ExternalInput" if name in ext_in else "Internal")
        return P.dram(name, shape, dt, kind=kind)

    x_in = din("x", [L, D])
    ident_d = din("ident", [128, 128], BF16)
    w13_d = [din(f"w13_{i}", [KF, 128, 2 * KD * 128]) for i in (1, 2)]
    w2_d = [din(f"w2_{i}", [128, KF * D]) for i in (1, 2)]
    winfm_d = din("win_fm", [20, 128, KD * 128])
    wintm_d = din("win_tm", [2, 128, KD * 512])
    wout_d = din("wout", [128, 8 * D])
    gpre_d = {k: din(k, [128, KD]) for k in ("ffn1_pre_g", "mix_pre_g", "ffn2_pre_g")}
    gpost_d = {k: din(k, [1, D]) for k in ("ffn1_post_g", "mix_post_g", "ffn2_post_g")}
    shortw_d = din("short_w", [128, NHC, 3])
    shortb_d = din("short_b", [128, NHC])
    rot_d = din("rot", [NB, 128, 2, T])
    pswap_d = din("pswap", [128, 128])
    flag_d = din("pflag", [128, 1])
    zext_d = din("zext", [33, 2, 16384], BF16)
    text_d = din("text", [1, 16384])
    negd_d = din("negd", [128, 4])
    fw1_d = din("filt_w1", [33, 64])
    fw2_d = din("filt_w2", [64, 64])
    fw3_d = din("filt_w3", [64, 64])
    fw4_d = din("filt_w4", [64, 2048])
    fb_d = din("filt_b", [64, 3])
    ffr_d = din("filt_freq", [64, 1])
    hbrow_d = din("hbias_row", [1, 1024])
    F1d_d = din("F1d", [64, 256], BF16)
    F1k_d = din("F1k", [128, 256], BF16)
    GT_d = din("GT", [128, 128, 384], BF16)
    F3_d = din("F3", [128, 2, 256], BF16)
    GI_d = din("GI", [128, 128, 128], BF16)
    lgf_d = din("ret_log_decay_f", [1, 4])
    lgb_d = din("ret_log_decay_b", [1, 4])
    rtab_d = din("rtab", [128, 4, 128])
    itab_d = din("itab", [128, 2, 128])
    jtab_d = din("jtab", [128, 3])

    out_d = P.dram("out", [L, D], F32, kind="ExternalOutput")
    w13b = [dscr(f"w13b_{i}", [KF, 128, 2 * KD * 128], BF16) for i in (1, 2)]
    winfmb = dscr("winfmb", [20, 128, KD * 128], BF16)
    X1 = dscr("X1", [L, D], F32)
    UC = dscr("UC", [NHC * 128, L + 1], F32)
    QT = dscr("QT", [4, 128, L], BF16)
    KTs = dscr("KT", [4, 128, L], BF16)
    KM = dscr("KM", [4, 128, L // 128, 128], BF16)
    VM = dscr("VM", [4, 128, L // 128, 128], BF16)
    GM = dscr("GM", [L, 512], F32)
    YT = dscr("YT", [D, L], BF16)
    X2 = dscr("X2", [L, D], F32)
    Z1F = dscr("Z1F", [512, L], F32)
    KTM = dscr("KTM", [2, 512, 16384], BF16)
    KSP = dscr("KSP", [2, 4, 32, 128, 2, 512], BF16)

    ident = P.sbuf("ident_s", [128, 128], BF16)
    P.dma("sp", ident[:], ident_d[:], reads=[ident_d], writes=[ident])
    gpre = {}
    for k, v in gpre_d.items():
        gpre[k] = P.sbuf(k + "_s", [128, KD], F32)
        P.dma("sp", gpre[k][:], v[:], reads=[v], writes=[gpre[k]])
    def load_gpost(k, half):
        t = P.sbuf(k + "_s", [128, D], F32)
        P.dma("sp", t[:], gpost_d[k].h.partition_broadcast(128).rearrange("p o d -> p (o d)"), reads=[gpost_d[k]], writes=[t])
        if half:
            P.op("pool", lambda e: e.tensor_scalar(out=t[:], in0=t[:], scalar1=0.5, scalar2=None, op0=ALU.mult), reads=[t], writes=[t])
        return t

    banks = [P.psum(f"bank{i}", [128, 512], F32) for i in range(8)]
    rot4 = Ring(banks[0:4])
    ysets = [(banks[4], banks[5]), (banks[6], banks[7])]

    xring = hnring = hn2ring = hTring = aT = w13ring = w2s = sgring = tmpring = xoring = None
    junk = P.sbuf("junk", [128, D], BF16)
    string = Ring([P.sbuf(f"st{i}", [128, 16], F32) for i in range(3)])
    s2ring = Ring([P.sbuf(f"s2{i}", [128, 8], F32) for i in range(4)])

    def alloc_ffn():
        nonlocal xring, hnring, hn2ring, hTring, aT, w13ring, w2s, sgring, tmpring, xoring
        xring = Ring([P.sbuf(f"xr{i}", [128, D], F32) for i in range(3)])
        hnring = Ring([P.sbuf(f"hn{i}", [128, D], BF16) for i in range(4)])
        hn2ring = Ring([P.sbuf(f"hnb{i}", [128, D], BF16) for i in range(4)])
        hTring = Ring([P.sbuf(f"hT{i}", [128, KD, T], BF16) for i in range(3)])
        aT = [P.sbuf(f"aT{i}", [128, T], BF16) for i in range(KF)]
        w13ring = Ring([P.sbuf(f"w13s{i}", [128, 2, KD, 128], BF16) for i in range(3)])
        w2s = P.sbuf("w2s", [128, KF, D], BF16)
        sgring = Ring([P.sbuf(f"sg{i}", [128, T], F32) for i in range(2)])
        tmpring = Ring([P.sbuf(f"tmp{i}", [128, D], F32) for i in range(1)])
        xoring = Ring([P.sbuf(f"xo{i}", [128, D], F32) for i in range(2)])

    def cast_weights(i):
        for fc in range(KF):
            P.dma("pool", w13b[i][fc], w13_d[i][fc], reads=[w13_d[i]], writes=[w13b[i]])

    def load_w2(i):
        for j in range(KF):
            P.dma("pool", w2s[:, j, :], w2_d[i][:, j * D:(j + 1) * D], reads=[w2_d[i]], writes=[w2s])

    def prep_tile(xt, st, col, gkey):
        P.op("act", lambda e: e.activation(out=junk[:], in_=xt[:], func=AF.Square, accum_out=st[:, col:col + 1]),
             reads=[xt], writes=[st])

    def rstd_cols(st, c0, n):
        P.op("act", lambda e: e.activation(out=st[:, 8 + c0:8 + c0 + n], in_=st[:, c0:c0 + n], func=AF.Sqrt, bias=EPS, scale=1.0 / D),
             reads=[st], writes=[st])
        P.op("dve", lambda e: e.reciprocal(out=st[:, 8 + c0:8 + c0 + n], in_=st[:, 8 + c0:8 + c0 + n]), reads=[st], writes=[st])

    def norm_cast(xt, st, col, ring=None):
        hn = (ring or hnring).next()
        P.op("act", lambda e: e.activation(out=hn[:], in_=xt[:], func=AF.Copy, scale=st[:, 8 + col:9 + col]),
             reads=[xt, st], writes=[hn])
        return hn

    def transpose_tile(hn, hT, tt, g):
        bk = rot4.next()
        bv = bk[:].bitcast(BF16)
        for kd in range(KD):
            P.op("pe", lambda e, kd=kd: e.transpose(out=bv[:, kd * 128:(kd + 1) * 128], in_=hn[:, kd * 128:(kd + 1) * 128], identity=ident[:]),
                 reads=[hn, ident], writes=[bk], pe_acc=True)
        P.op("dve", lambda e: e.tensor_tensor(out=hT[:, :, tt * 128:(tt + 1) * 128],
                                              in0=bv[:, 0:KD * 128].rearrange("p (k t) -> p k t", k=KD),
                                              in1=g[:].unsqueeze(2).to_broadcast([128, KD, 128]), op=ALU.mult),
             reads=[bk, g], writes=[hT])

    def ffn_s3(hT, w13bi):
        for fc in range(KF):
            ws = w13ring.next()
            P.dma("sp", ws[:].rearrange("p a k f -> p (a k f)"), w13bi[fc], reads=[w13bi], writes=[ws])
            pg = rot4.next()
            pu = rot4.next()
            for which, pb in ((0, pg), (1, pu)):
                for kd in range(KD):
                    P.op("pe", lambda e, which=which, pb=pb, kd=kd, ws=ws: e.matmul(pb[:, 0:T], lhsT=ws[:, which, kd, :], rhs=hT[:, kd, :],
                                                                                   start=(kd == 0), stop=(kd == KD - 1)),
                         reads=[ws, hT], writes=[pb], pe_acc=True)
            sg = sgring.next()
            P.op("act", lambda e, sg=sg, pg=pg: e.activation(out=sg[:], in_=pg[:, 0:T], func=AF.Silu), reads=[pg], writes=[sg])
            P.op("dve", lambda e, sg=sg, pu=pu, fc=fc: e.tensor_tensor(out=aT[fc][:], in0=sg[:], in1=pu[:, 0:T], op=ALU.mult),
                 reads=[sg, pu], writes=[aT[fc]])

    def mm_tokmajor(ys, lhs_list, rhs_fn, reads_fn):
        n = len(lhs_list)
        for dh in range(2):
            for i, lh in enumerate(lhs_list):
                P.op("pe", lambda e, dh=dh, i=i, lh=lh: e.matmul(ys[dh][:, 0:512], lhsT=lh, rhs=rhs_fn(i, dh), start=(i == 0), stop=(i == n - 1)),
                     reads=reads_fn(i), writes=[ys[dh]], pe_acc=True)

    def epilogue(ys, xres, gp, xo):
        s2 = s2ring.next()
        for dh in range(2):
            P.op("act", lambda e, dh=dh: e.activation(out=junk[:, 0:512], in_=ys[dh][:, 0:512], func=AF.Square, accum_out=s2[:, dh:dh + 1]),
                 reads=[ys[dh]], djw=[s2])
        P.op("dve", lambda e: e.tensor_tensor(out=s2[:, 2:3], in0=s2[:, 0:1], in1=s2[:, 1:2], op=ALU.add), reads=[s2], writes=[s2])
        P.op("act", lambda e: e.activation(out=s2[:, 3:4], in_=s2[:, 2:3], func=AF.Sqrt, bias=EPS, scale=1.0 / D), reads=[s2], writes=[s2])
        P.op("dve", lambda e: e.reciprocal(out=s2[:, 4:5], in_=s2[:, 3:4]), reads=[s2], writes=[s2])
        tmp = tmpring.next()
        for dh in range(2):
            P.op("dve", lambda e, dh=dh: e.scalar_tensor_tensor(out=tmp[:, dh * 512:(dh + 1) * 512], in0=ys[dh][:, 0:512], scalar=s2[:, 4:5],
                                                                 in1=gp[:, dh * 512:(dh + 1) * 512], op0=ALU.mult, op1=ALU.mult),
                 reads=[ys[dh], s2, gp], writes=[tmp])
        P.op("pool", lambda e: e.tensor_tensor(out=xo[:], in0=tmp[:], in1=xres[:], op=ALU.add), reads=[tmp, xres], writes=[xo])

    def prep_pe(hns, gkey):
        hT = hTring.next()
        for tt in range(TT):
            transpose_tile(hns[tt], hT, tt, gpre[gkey])
        return hT

    if "A" in phases:
        P.push_scope()
        alloc_ffn()
        gpost1 = load_gpost("ffn1_post_g", True)
        cast_weights(0)
        cast_weights(1)
        load_w2(0)
        for c in range(20):
            P.dma("pool", winfmb[c], winfm_d[c], reads=[winfm_d], writes=[winfmb])
        wtm = P.sbuf("wtm", [128, 2, KD, 512], BF16)
        for i in range(2):
            for kd in range(KD):
                P.dma("pool", wtm[:, i, kd, :], wintm_d[i][:, kd * 512:(kd + 1) * 512], reads=[wintm_d], writes=[wtm])
        shw = P.sbuf("shw", [128, NHC, 3], F32)
        shb = P.sbuf("shb", [128, NHC], F32)
        P.dma("sp", shw[:], shortw_d[:], reads=[shortw_d], writes=[shw])
        P.dma("sp", shb[:], shortb_d[:], reads=[shortb_d], writes=[shb])
        pswap = P.sbuf("pswap_s", [128, 128], BF16)
        P.dma("pool", pswap[:], pswap_d[:], reads=[pswap_d], writes=[pswap])
        flag = P.sbuf("flag_s", [128, 1], F32)
        P.dma("sp", flag[:], flag_d[:], reads=[flag_d], writes=[flag])
        fw = P.sbuf("fw", [128, NHC, 3], F32)
        P.op("dve", lambda e: e.tensor_scalar(out=fw[:], in0=shw[:], scalar1=flag[:, 0:1], scalar2=-1.0, op0=ALU.mult, op1=ALU.mult),
             reads=[shw, flag], writes=[fw])
        saved = P.sbuf("saved", [128, NHC, 2], F32)
        P.op("pool", lambda e: e.memset(saved[:], 0.0), writes=[saved])
        wfmring = Ring([P.sbuf(f"wfm{i}", [128, KD, 128], BF16) for i in range(3)])
        Sring = Ring([P.sbuf(f"S{i}", [128, T + 2], F32) for i in range(3)])
        accring = Ring([P.sbuf(f"acc{i}", [128, T], F32) for i in range(3)])
        qsring = Ring([P.sbuf(f"qs{i}", [128, T], BF16) for i in range(2)])
        r2ring = Ring([P.sbuf(f"r2{i}", [128, T], F32) for i in range(1)])
        r1ring = Ring([P.sbuf(f"r1{i}", [128, T], F32) for i in range(1)])
        qoring = Ring([P.sbuf(f"qo{i}", [128, T], BF16) for i in range(2)])
        rotring = Ring([P.sbuf(f"rot{i}", [128, 2, T], F32) for i in range(1)])
        kmring = Ring([P.sbuf(f"km{i}", [128, TT, 128], BF16) for i in range(2)])
        vgring = Ring([P.sbuf(f"vg{i}", [128, 512], BF16) for i in range(2)])
        ggring = Ring([P.sbuf(f"gg{i}", [128, 512], F32) for i in range(1)])

        def prep_act(b):
            st = string.next()
            hns = []
            for tt in range(TT):
                xt = xring.next()
                P.dma("sp", xt[:], x_in[b * T + tt * 128: b * T + (tt + 1) * 128, :], reads=[x_in], writes=[xt])
                prep_tile(xt, st, tt, None)
                rstd_cols(st, tt, 1)
                hns.append(norm_cast(xt, st, tt))
            return hns

        rot8 = Ring(banks)

        def s5(b, h2T):
            rot4 = rot8
            rt = rotring.next()
            P.dma("sp", rt[:], rot_d[b], reads=[rot_d], writes=[rt])
            for cc in range(NHC):
                wf = wfmring.next()
                P.dma("sp", wf[:].rearrange("p k f -> p (k f)"), winfmb[cc], reads=[winfmb], writes=[wf])
                pb = rot4.next()
                for kd in range(KD):
                    P.op("pe", lambda e, kd=kd, wf=wf, pb=pb: e.matmul(pb[:, 0:T], lhsT=wf[:, kd, :], rhs=h2T[:, kd, :], start=(kd == 0), stop=(kd == KD - 1)),
                         reads=[wf, h2T], writes=[pb], pe_acc=True)
                S = Sring.next()
                P.op("act", lambda e, S=S, pb=pb: e.activation(out=S[:, 2:T + 2], in_=pb[:, 0:T], func=AF.Copy), reads=[pb], writes=[S])
                P.op("pool", lambda e, S=S, cc=cc: e.tensor_copy(out=S[:, 0:2], in_=saved[:, cc, :]), reads=[saved], writes=[S])
                acc = accring.next()
                P.op("dve", lambda e, S=S, cc=cc, acc=acc: e.tensor_scalar(out=acc[:], in0=S[:, 1:T + 1], scalar1=shw[:, cc, 1:2], scalar2=shb[:, cc:cc + 1],
                                                                         op0=ALU.mult, op1=ALU.add), reads=[S, shw, shb], writes=[acc])
                P.op("dve", lambda e, S=S, cc=cc, acc=acc: e.scalar_tensor_tensor(out=acc[:], in0=S[:, 0:T], scalar=shw[:, cc, 0:1], in1=acc[:],
                                                                                  op0=ALU.mult, op1=ALU.add), reads=[S, shw, acc], writes=[acc])
                P.op("dve", lambda e, S=S, cc=cc, acc=acc: e.scalar_tensor_tensor(out=acc[:], in0=S[:, 2:T + 2], scalar=shw[:, cc, 2:3], in1=acc[:],
                                                                                 op0=ALU.mult, op1=ALU.add), reads=[S, shw, acc], writes=[acc])
                if b == NB // 2:
                    P.op("dve", lambda e, S=S, cc=cc, acc=acc: e.scalar_tensor_tensor(out=acc[:, 1:2], in0=S[:, 2:3], scalar=fw[:, cc, 2:3], in1=acc[:, 1:2],
                                                                                     op0=ALU.mult, op1=ALU.add), reads=[S, fw, acc], writes=[acc])
                    P.op("dve", lambda e, S=S, cc=cc, acc=acc: e.scalar_tensor_tensor(out=acc[:, 2:3], in0=S[:, 1:2], scalar=fw[:, cc, 0:1], in1=acc[:, 2:3],
                                                                                     op0=ALU.mult, op1=ALU.add), reads=[S, fw, acc], writes=[acc])
                P.op("pool", lambda e, S=S, cc=cc: e.tensor_copy(out=saved[:, cc, :], in_=S[:, T:T + 2]), reads=[S], writes=[saved])
                P.dma("pool", UC[cc * 128:(cc + 1) * 128, b * T:(b + 1) * T], acc[:], reads=[acc], writes=[], djw=[UC])
            for i in range(8):
                isk = i >= 4
                h = i % 4
                wf = wfmring.next()
                P.dma("sp", wf[:].rearrange("p k f -> p (k f)"), winfmb[NHC + i], reads=[winfmb], writes=[wf])
                pb = rot4.next()
                for kd in range(KD):
                    P.op("pe", lambda e, kd=kd, wf=wf, pb=pb: e.matmul(pb[:, 0:T], lhsT=wf[:, kd, :], rhs=h2T[:, kd, :], start=(kd == 0), stop=(kd == KD - 1)),
                         reads=[wf, h2T], writes=[pb], pe_acc=True)
                qs = qsring.next()
                P.op("act", lambda e, qs=qs, pb=pb, isk=isk: e.activation(out=qs[:], in_=pb[:, 0:T], func=AF.Copy, scale=(128 ** -0.5 if isk else 1.0)),
                     reads=[pb], writes=[qs])
                pb2 = rot4.next()
                P.op("pe", lambda e, qs=qs, pb2=pb2: e.matmul(pb2[:, 0:T], lhsT=pswap[:], rhs=qs[:], start=True, stop=True),
                     reads=[pswap, qs], writes=[pb2], pe_acc=True)
                r1 = r1ring.next()
                P.op("pool", lambda e, qs=qs, r1=r1, rt=rt: e.tensor_tensor(out=r1[:], in0=qs[:], in1=rt[:, 0, :], op=ALU.mult), reads=[qs, rt], writes=[r1])
                r2 = r2ring.next()
                P.op("dve", lambda e, pb2=pb2, r2=r2, rt=rt: e.tensor_tensor(out=r2[:], in0=pb2[:, 0:T], in1=rt[:, 1, :], op=ALU.mult), reads=[pb2, rt], writes=[r2])
                qo = qoring.next()
                P.op("dve", lambda e, r1=r1, r2=r2, qo=qo: e.tensor_tensor(out=qo[:], in0=r1[:], in1=r2[:], op=ALU.add), reads=[r1, r2], writes=[qo])
                dst = KTs if isk else QT
                P.dma("pool", dst[h][:, b * T:(b + 1) * T], qo[:], reads=[qo], writes=[], djw=[dst])
                if isk:
                    bk = rot4.next()
                    bv = bk[:].bitcast(BF16)
                    for tt in range(TT):
                        P.op("pe", lambda e, tt=tt, qo=qo, bv=bv: e.transpose(out=bv[:, tt * 128:(tt + 1) * 128], in_=qo[:, tt * 128:(tt + 1) * 128], identity=ident[:]),
                             reads=[qo, ident], writes=[bk], pe_acc=True)
                    km = kmring.next()
                    P.op("act", lambda e, km=km, bv=bv: e.activation(out=km[:].rearrange("p a d -> p (a d)"), in_=bv[:, 0:TT * 128], func=AF.Copy),
                         reads=[bk], writes=[km])
                    P.dma("pool", KM[h][:, 4 * b:4 * b + 4, :], km[:], reads=[km], writes=[], djw=[KM])
            for tt in range(TT):
                for which in range(2):
                    pb = rot4.next()
                    for kd in range(KD):
                        P.op("pe", lambda e, kd=kd, pb=pb, which=which, tt=tt: e.matmul(pb[:, 0:512], lhsT=h2T[:, kd, tt * 128:(tt + 1) * 128], rhs=wtm[:, which, kd, :],
                                                                                     start=(kd == 0), stop=(kd == KD - 1)),
                             reads=[h2T, wtm], writes=[pb], pe_acc=True)
                    r0 = b * T + tt * 128
                    if which == 0:
                        vt = vgring.next()
                        P.op("act", lambda e, vt=vt, pb=pb: e.activation(out=vt[:], in_=pb[:, 0:512], func=AF.Copy), reads=[pb], writes=[vt])
                        P.dma("pool", VM[:, :, 4 * b + tt, :].rearrange("h p e -> p h e"), vt[:].rearrange("p (h e) -> p h e", h=4), reads=[vt], writes=[], djw=[VM])
                    else:
                        gt = ggring.next()
                        P.op("act", lambda e, gt=gt, pb=pb: e.activation(out=gt[:], in_=pb[:, 0:512], func=AF.Silu), reads=[pb], writes=[gt])
                        P.dma("pool", GM[r0:r0 + 128, :], gt[:], reads=[gt], writes=[], djw=[GM])

        hT_cur = prep_pe(prep_act(0), "ffn1_pre_g")
        pend = None
        for b in range(nb):
            ffn_s3(hT_cur, w13b[0])
            if pend is not None:
                s5(pend[0], prep_pe(pend[1], "mix_pre_g"))
            if b + 1 < nb:
                hns_next = prep_act(b + 1)
            st2 = string.next()
            hn2s = []
            for tt in range(TT):
                if tt == TT - 1 and b + 1 < nb:
                    hT_cur = prep_pe(hns_next, "ffn1_pre_g")
                ys = ysets[tt % 2]
                mm_tokmajor(ys, [aT[fc][:, tt * 128:(tt + 1) * 128] for fc in range(KF)],
                            lambda i, dh: w2s[:, i, dh * 512:(dh + 1) * 512], lambda i: [aT[i], w2s])
                xt = xring.next()
                r0 = b * T + tt * 128
                P.dma("sp", xt[:], x_in[r0:r0 + 128, :], reads=[x_in], writes=[xt])
                xo = xoring.next()
                epilogue(ys, xt, gpost1, xo)
                P.dma("pool", X1[r0:r0 + 128, :], xo[:], reads=[xo], writes=[], djw=[X1])
                prep_tile(xo, st2, tt, None)
                rstd_cols(st2, tt, 1)
                hn2s.append(norm_cast(xo, st2, tt, hn2ring))
            pend = (b, hn2s)
        s5(pend[0], prep_pe(pend[1], "mix_pre_g"))
        for cc in range(NHC):
            a1 = accring.next()
            P.op("dve", lambda e, cc=cc, a1=a1: e.tensor_scalar(out=a1[:, 0:1], in0=saved[:, cc, 1:2], scalar1=shw[:, cc, 1:2], scalar2=shb[:, cc:cc + 1],
                                                               op0=ALU.mult, op1=ALU.add), reads=[saved, shw, shb], writes=[a1])
            P.op("dve", lambda e, cc=cc, a1=a1: e.scalar_tensor_tensor(out=a1[:, 0:1], in0=saved[:, cc, 0:1], scalar=shw[:, cc, 0:1], in1=a1[:, 0:1],
                                                                      op0=ALU.mult, op1=ALU.add), reads=[saved, shw, a1], writes=[a1])
            P.dma("pool", UC[cc * 128:(cc + 1) * 128, nb * T:nb * T + 1], a1[:, 0:1], reads=[a1], writes=[UC], allow_slow_non_contiguous=True)
        P.pop_scope()

    if "B" in phases:
        P.push_scope()
        allb = Ring(banks)
        PI = math.pi
        P.push_scope()
        fw1 = P.sbuf("fw1", [33, 64], F32)
        fw2 = P.sbuf("fw2", [64, 64], F32)
        fw3 = P.sbuf("fw3", [64, 64], F32)
        fw4 = P.sbuf("fw4", [64, 2048], F32)
        fbs = P.sbuf("fbs", [64, 3], F32)
        ffr = P.sbuf("ffr", [64, 1], F32)
        negd = P.sbuf("negd_s", [128, 4], F32)
        for t_, d_ in ((fw1, fw1_d), (fw2, fw2_d), (fw3, fw3_d), (fw4, fw4_d), (fbs, fb_d), (ffr, ffr_d), (negd, negd_d)):
            P.dma("sp", t_[:], d_[:], reads=[d_], writes=[t_])
        frb = P.sbuf("frb", [64, 3], F32)
        P.op("dve", lambda e: e.tensor_scalar(out=frb[:], in0=fbs[:], scalar1=ffr[:, 0:1], scalar2=None, op0=ALU.mult), reads=[fbs, ffr], writes=[frb])
        h3h = P.sbuf("h3hi", [64, 16384], BF16)
        h3l = P.sbuf("h3lo", [64, 16384], BF16)

        def split(src_ap, hi_ap, lo_ap, reads, hi_t, lo_t, dj=False):
            kw_h = dict(djw=[hi_t]) if dj else dict(writes=[hi_t])
            kw_l = dict(djw=[lo_t]) if dj else dict(writes=[lo_t])
            P.op("act", lambda e: e.activation(out=hi_ap, in_=src_ap, func=AF.Copy), reads=reads, **kw_h)
            P.op("dve", lambda e: e.tensor_tensor(out=lo_ap, in0=src_ap, in1=hi_ap, op=ALU.subtract), reads=reads + [hi_t], **kw_l)

        wsp = {}
        for nm, wt, shp in (("w1", fw1, [33, 64]), ("w2", fw2, [64, 64]), ("w3", fw3, [64, 64]), ("w4", fw4, [64, 2048])):
            hi_t = P.sbuf(nm + "hi", shp, BF16)
            lo_t = P.sbuf(nm + "lo", shp, BF16)
            split(wt[:], hi_t[:], lo_t[:], [wt], hi_t, lo_t)
            wsp[nm] = (hi_t, lo_t)
        P.push_scope()
        zhring = Ring([P.sbuf(f"zh{i}", [33, 2, 512], BF16) for i in range(8)])
        hring = Ring([P.sbuf(f"hh{i}", [64, 512], F32) for i in range(8)])
        hring2 = Ring([P.sbuf(f"hg{i}", [64, 512], F32) for i in range(8)])
        hsring = Ring([P.sbuf(f"hs{i}", [64, 2, 512], BF16) for i in range(8)])

        def mm3(out_ap, wkey, c0, c1, xh, xl, reads, bk):
            wh, wl = wsp[wkey]
            P.op("pe", lambda e: e.matmul(out_ap, lhsT=wh[:, c0:c1], rhs=xh, start=True, stop=False), reads=reads + [wh], writes=[bk], pe_acc=True)
            P.op("pe", lambda e: e.matmul(out_ap, lhsT=wh[:, c0:c1], rhs=xl, start=False, stop=False), reads=reads + [wh], writes=[bk], pe_acc=True)
            P.op("pe", lambda e: e.matmul(out_ap, lhsT=wl[:, c0:c1], rhs=xh, start=False, stop=True), reads=reads + [wl], writes=[bk], pe_acc=True)

        GB = 4
        for bg in range(32 // GB):
            blks = [bg * GB + i for i in range(GB)]
            cur = []
            for blk in blks:
                zt = zhring.next()
                P.dma("sp", zt[:], zext_d[:, :, blk * 512:(blk + 1) * 512], reads=[zext_d], writes=[zt])
                cur.append((zt[:, 0, :], zt[:, 1, :], [zt]))
            for li, wkey in enumerate(("w1", "w2", "w3")):
                bks, hts, has, hbs = [], [], [], []
                for i in range(GB):
                    bk = allb.next()
                    mm3(bk[0:64, 0:512], wkey, 0, 64, cur[i][0], cur[i][1], cur[i][2], bk)
                    bks.append(bk)
                for i in range(GB):
                    ht = hring.next()
                    hts.append(ht)
                    P.op("dve", lambda e, bk=bks[i], ht=ht, li=li: e.tensor_scalar(out=ht[:], in0=bk[0:64, 0:512], scalar1=ffr[:, 0:1], scalar2=frb[:, li:li + 1], op0=ALU.mult, op1=ALU.add),
                         reads=[bks[i], ffr, frb], writes=[ht])
                for i in range(GB):
                    ha, hb_ = hring2.next(), hring2.next()
                    has.append(ha)
                    hbs.append(hb_)
                    P.op("dve", lambda e, ht=hts[i], ha=ha: e.tensor_scalar(out=ha[:], in0=ht[:], scalar1=PI, scalar2=-2.0 * PI, op0=ALU.is_gt, op1=ALU.mult), reads=[hts[i]], writes=[ha])
                    P.op("dve", lambda e, ht=hts[i], hb_=hb_: e.tensor_scalar(out=hb_[:], in0=ht[:], scalar1=-PI, scalar2=2.0 * PI, op0=ALU.is_lt, op1=ALU.mult), reads=[hts[i]], writes=[hb_])
                for i in range(GB):
                    P.op("dve", lambda e, ha=has[i], hb_=hbs[i]: e.tensor_tensor(out=ha[:], in0=ha[:], in1=hb_[:], op=ALU.add), reads=[has[i], hbs[i]], writes=[has[i]])
                for i in range(GB):
                    P.op("dve", lambda e, ht=hts[i], ha=has[i]: e.tensor_tensor(out=ht[:], in0=ht[:], in1=ha[:], op=ALU.add), reads=[hts[i], has[i]], writes=[hts[i]])
                for i in range(GB):
                    P.op("act", lambda e, ht=hts[i]: e.activation(out=ht[:], in_=ht[:], func=AF.Sin), reads=[hts[i]], writes=[hts[i]])
                nxt = []
                if li < 2:
                    hss = [hsring.next() for _ in range(GB)]
                    for i in range(GB):
                        P.op("act", lambda e, ht=hts[i], hs=hss[i]: e.activation(out=hs[:, 0, :], in_=ht[:], func=AF.Copy), reads=[hts[i]], writes=[hss[i]])
                    for i in range(GB):
                        P.op("dve", lambda e, ht=hts[i], hs=hss[i]: e.tensor_tensor(out=hs[:, 1, :], in0=ht[:], in1=hs[:, 0, :], op=ALU.subtract), reads=[hts[i], hss[i]], djw=[hss[i]])
                        nxt.append((hss[i][:, 0, :], hss[i][:, 1, :], [hss[i]]))
                    cur = nxt
                else:
                    for i, blk in enumerate(blks):
                        sl = slice(blk * 512, (blk + 1) * 512)
                        P.op("act", lambda e, ht=hts[i], sl=sl: e.activation(out=h3h[:, sl], in_=ht[:], func=AF.Copy), reads=[hts[i]], djw=[h3h])
                    for i, blk in enumerate(blks):
                        sl = slice(blk * 512, (blk + 1) * 512)
                        P.op("dve", lambda e, ht=hts[i], sl=sl: e.tensor_tensor(out=h3l[:, sl], in0=ht[:], in1=h3h[:, sl], op=ALU.subtract), reads=[hts[i], h3h], djw=[h3l])
        P.pop_scope()
        ktb = [P.sbuf(f"ktb{n}", [128, 16384], BF16) for n in range(2)]
        tbring = Ring([P.sbuf(f"tb{i}", [128, 2048], F32) for i in range(2)])
        wring = Ring([P.sbuf(f"wn{i}", [128, 512], F32) for i in range(8)])
        kfring = Ring([P.sbuf(f"kx{i}", [128, 512], F32) for i in range(8)])
        nrm = P.sbuf("nrm", [128, 2, 40], F32)
        for cc in range(4):
            for bg in range(8):
                blks = [bg * 4 + i for i in range(4)]
                dr = blks[0] // 16
                tb = tbring.next()
                P.dma("sp", tb[:], text_d[:, bg * 2048:(bg + 1) * 2048].partition_broadcast(128).rearrange("p o d -> p (o d)"), reads=[text_d], writes=[tb])
                wns = []
                for i in range(4):
                    wn = wring.next()
                    wns.append(wn)
                    P.op("act", lambda e, wn=wn, tb=tb, cc=cc, i=i: e.activation(out=wn[:], in_=tb[:, i * 512:(i + 1) * 512], func=AF.Exp, scale=negd[:, cc:cc + 1]),
                         reads=[tb, negd], writes=[wn])
                for n in range(2):
                    col0 = n * 1024 + dr * 512 + cc * 128
                    bks, kxs = [], []
                    for i, blk in enumerate(blks):
                        bk = allb.next()
                        bks.append(bk)
                        sl = slice(blk * 512, (blk + 1) * 512)
                        mm3(bk[:, 0:512], "w4", col0, col0 + 128, h3h[:, sl], h3l[:, sl], [h3h, h3l], bk)
                    for i in range(4):
                        kx = kfring.next()
                        kxs.append(kx)
                        P.op("dve", lambda e, kx=kx, bk=bks[i], wn=wns[i]: e.tensor_tensor(out=kx[:], in0=bk[:, 0:512], in1=wn[:], op=ALU.mult), reads=[bks[i], wns[i]], writes=[kx])
                    for i, blk in enumerate(blks):
                        P.op("dve", lambda e, kx=kxs[i], n=n, blk=blk: e.tensor_reduce(out=nrm[:, n, blk:blk + 1], in_=kx[:], axis=AX.X, op=ALU.add, apply_absolute_value=True),
                             reads=[kxs[i]], djw=[nrm])
                    for i, blk in enumerate(blks):
                        P.op("act", lambda e, kx=kxs[i], n=n, blk=blk: e.activation(out=ktb[n][:, blk * 512:(blk + 1) * 512], in_=kx[:], func=AF.Copy), reads=[kxs[i]], djw=[ktb[n]])
            for n in range(2):
                bk = allb.next()
                colb = n * 1024 + 512 + cc * 128
                mm3(bk[:, 0:1], "w4", colb, colb + 128, h3h[:, 0:1], h3l[:, 0:1], [h3h, h3l], bk)
                P.op("act", lambda e, bk=bk, n=n: e.activation(out=nrm[:, n, 32:33], in_=bk[:, 0:1], func=AF.Abs), reads=[bk], djw=[nrm])
                P.op("dve", lambda e, n=n: e.tensor_reduce(out=nrm[:, n, 34:35], in_=nrm[:, n, 0:16], axis=AX.X, op=ALU.add), reads=[nrm], djw=[nrm])
                P.op("dve", lambda e, n=n: e.tensor_reduce(out=nrm[:, n, 35:36], in_=nrm[:, n, 16:33], axis=AX.X, op=ALU.add), reads=[nrm], djw=[nrm])
                P.op("dve", lambda e, n=n: e.reciprocal(out=nrm[:, n, 36:38], in_=nrm[:, n, 34:36]), reads=[nrm], djw=[nrm])
                P.op("dve", lambda e, n=n: e.tensor_scalar(out=ktb[n][:, 0:8192], in0=ktb[n][:, 0:8192], scalar1=nrm[:, n, 36:37], scalar2=None, op0=ALU.mult),
                     reads=[ktb[n], nrm], djw=[ktb[n]])
                P.op("act", lambda e, n=n: e.activation(out=ktb[n][:, 8192:16384], in_=ktb[n][:, 8192:16384], func=AF.Copy, scale=nrm[:, n, 37:38]),
                     reads=[ktb[n], nrm], djw=[ktb[n]])
                P.dma("pool", KTM[n][cc * 128:(cc + 1) * 128, :], ktb[n][:], reads=[ktb[n]], writes=[], djw=[KTM])
        P.pop_scope()

        F1d = P.sbuf("F1d_s", [64, 256], BF16)
        F1k = P.sbuf("F1k_s", [128, 256], BF16)
        F3 = P.sbuf("F3_s", [128, 2, 256], BF16)
        for t_, d_ in ((F1d, F1d_d), (F1k, F1k_d), (F3, F3_d)):
            P.dma("sp", t_[:], d_[:], reads=[d_], writes=[t_])
        src = P.sbuf("fsrc", [128, 128, 128], BF16)
        Ybuf = P.sbuf("Ybuf", [128, 128, 2, 128], BF16)
        Pbuf = P.sbuf("Pbuf", [128, 2, 128, 128], BF16)
        gtring = Ring([P.sbuf(f"gt{i}", [128, 4, 384], BF16) for i in range(2)])
        giring = Ring([P.sbuf(f"gi{i}", [128, 4, 128], BF16) for i in range(2)])
        xrring = Ring([P.sbuf(f"xq{i}", [128, 2, 512], BF16) for i in range(2)])
        kfr = Ring([P.sbuf(f"kfs{i}", [128, 2, 512], BF16) for i in range(3)])
        tring = Ring([P.sbuf(f"tt{i}", [128, 512], F32) for i in range(6)])
        evi = [0]

        def evac(out_ap, in_ap, reads, writes, djw=()):
            eng = "act" if evi[0] % 2 == 0 else "dve"
            evi[0] += 1
            if eng == "act":
                P.op("act", lambda e: e.activation(out=out_ap, in_=in_ap, func=AF.Copy), reads=reads, writes=writes, djw=djw)
            else:
                P.op("dve", lambda e: e.tensor_copy(out=out_ap, in_=in_ap), reads=reads, writes=writes, djw=djw)

        NKG = 17
        KH = 68

        sguard = []

        def fft_fwd(A, F1, consumer):
            for c2 in range(64):
                bk = allb.next()
                for u in range(2):
                    c = 2 * c2 + u
                    P.op("pe", lambda e, bk=bk, u=u, c=c: e.matmul(bk[:, u * 256:(u + 1) * 256], lhsT=src[0:A, c, :], rhs=F1[0:A, :], start=True, stop=True),
                         reads=[src, F1] + sguard, writes=[bk], pe_acc=True)
                evac(Ybuf[:, 2 * c2:2 * c2 + 2, :, :].rearrange("p c r k -> p (c r k)"), bk[:, 0:512], [bk], [], djw=[Ybuf])
            for kg in range(NKG):
                gt = gtring.next()
                P.dma("sp", gt[:], GT_d[kg * 4:(kg + 1) * 4].rearrange("k b x -> b k x"), reads=[GT_d], writes=[gt])
                br = allb.next()
                bi = allb.next()
                for u in range(4):
                    k1 = kg * 4 + u
                    sl = slice(u * 128, (u + 1) * 128)
                    P.op("pe", lambda e, br=br, gt=gt, u=u, k1=k1, sl=sl: e.matmul(br[:, sl], lhsT=gt[:, u, 0:128], rhs=Ybuf[:, :, 0, k1], start=True, stop=False), reads=[gt, Ybuf], writes=[br], pe_acc=True)
                    P.op("pe", lambda e, br=br, gt=gt, u=u, k1=k1, sl=sl: e.matmul(br[:, sl], lhsT=gt[:, u, 256:384], rhs=Ybuf[:, :, 1, k1], start=False, stop=True), reads=[gt, Ybuf], writes=[br], pe_acc=True)
                    P.op("pe", lambda e, bi=bi, gt=gt, u=u, k1=k1, sl=sl: e.matmul(bi[:, sl], lhsT=gt[:, u, 128:256], rhs=Ybuf[:, :, 0, k1], start=True, stop=False), reads=[gt, Ybuf], writes=[bi], pe_acc=True)
                    P.op("pe", lambda e, bi=bi, gt=gt, u=u, k1=k1, sl=sl: e.matmul(bi[:, sl], lhsT=gt[:, u, 0:128], rhs=Ybuf[:, :, 1, k1], start=False, stop=True), reads=[gt, Ybuf], writes=[bi], pe_acc=True)
                consumer(kg, br, bi)

        def fft_inv(A):
            for c2 in range(64):
                bk = allb.next()
                for u in range(2):
                    c = 2 * c2 + u
                    P.op("pe", lambda e, bk=bk, u=u, c=c: e.matmul(bk[0:KH, u * 256:(u + 1) * 256], lhsT=Pbuf[:, 0, 0:KH, c], rhs=F3[:, 0, :], start=True, stop=False), reads=[Pbuf, F3], writes=[bk], pe_acc=True)
                    P.op("pe", lambda e, bk=bk, u=u, c=c: e.matmul(bk[0:KH, u * 256:(u + 1) * 256], lhsT=Pbuf[:, 1, 0:KH, c], rhs=F3[:, 1, :], start=False, stop=True), reads=[Pbuf, F3], writes=[bk], pe_acc=True)
                evac(Ybuf[0:KH, 2 * c2:2 * c2 + 2, :, :].rearrange("p c r k -> p (c r k)"), bk[0:KH, 0:512], [bk], [], djw=[Ybuf])
            for bg in range(32):
                gi = giring.next()
                P.dma("sp", gi[:], GI_d[bg * 4:(bg + 1) * 4].rearrange("b k x -> k b x"), reads=[GI_d], writes=[gi])
                bk = allb.next()
                for u in range(4):
                    b_ = bg * 4 + u
                    sl = slice(u * 128, (u + 1) * 128)
                    P.op("pe", lambda e, bk=bk, gi=gi, u=u, b_=b_, sl=sl: e.matmul(bk[0:A, sl], lhsT=gi[0:KH, u, 0:A], rhs=Ybuf[0:KH, :, 0, b_], start=True, stop=False), reads=[gi, Ybuf], writes=[bk], pe_acc=True)
                    P.op("pe", lambda e, bk=bk, gi=gi, u=u, b_=b_, sl=sl: e.matmul(bk[0:A, sl], lhsT=gi[0:KH, u, 64:64 + A], rhs=Ybuf[0:KH, :, 1, b_], start=False, stop=True), reads=[gi, Ybuf], writes=[bk], pe_acc=True)
                evac(src[0:A, :, bg * 4:(bg + 1) * 4].rearrange("p c b -> p b c"), bk[0:A, 0:512].rearrange("p (b c) -> p b c", b=4), [bk], [], djw=[src])

        for n in range(2):
            for cc in range(4):
                P.dma("sp", src[:], KTM[n][cc * 128:(cc + 1) * 128, :].rearrange("c (a b) -> a c b", b=128), reads=[KTM], writes=[src])

                def store_kf(kg, br, bi, n=n, cc=cc):
                    xs = xrring.next()
                    evac(xs[:, 0, :], br[:, 0:512], [br], [xs])
                    evac(xs[:, 1, :], bi[:, 0:512], [bi], [], djw=[xs])
                    P.dma("pool", KSP[n][cc][kg], xs[:], reads=[xs], writes=[], djw=[KSP])
                fft_fwd(128, F1k, store_kf)

        CG = 16
        hbb = P.sbuf("hbb", [64, 2, 512], F32)
        P.dma("sp", hbb[:].rearrange("p n c -> p (n c)"), hbrow_d.h.partition_broadcast(64).rearrange("p o d -> p (o d)"), reads=[hbrow_d], writes=[hbb])
        pflat = Pbuf[:].rearrange("p r k c -> p (r k c)").bitcast(F32)
        galias = [Tl(pflat[0:64, i * CG * 128:(i + 1) * CG * 128].rearrange("p (c b) -> p c b", c=CG), f"ga{i}") for i in range(8)]
        srcg = Tl(None, "srcg")
        gxr = Ring(galias[0:4])
        gzr = Ring(galias[4:8])
        for cc in range(4):
            rows = slice(cc * 128, (cc + 1) * 128)
            P.dma("pool", src[0:64, :, :], UC[rows, 1:L + 1].rearrange("c (a b) -> a c b", b=128), reads=[UC], writes=[src, srcg])
            if not sguard:
                sguard.append(srcg)
            for n in range(2):
                def mult(kg, br, bi, n=n, cc=cc):
                    ks = kfr.next()
                    P.dma("sp", ks[:], KSP[n][cc][kg], reads=[KSP], writes=[ks])
                    t1, t2, t3, t4 = (tring.next() for _ in range(4))
                    P.op("dve", lambda e: e.tensor_tensor(out=t1[:], in0=br[:, 0:512], in1=ks[:, 0, :], op=ALU.mult), reads=[br, ks], writes=[t1])
                    P.op("dve", lambda e: e.tensor_tensor(out=t2[:], in0=bi[:, 0:512], in1=ks[:, 1, :], op=ALU.mult), reads=[bi, ks], writes=[t2])
                    P.op("dve", lambda e: e.tensor_tensor(out=t3[:], in0=br[:, 0:512], in1=ks[:, 1, :], op=ALU.mult), reads=[br, ks], writes=[t3])
                    P.op("dve", lambda e: e.tensor_tensor(out=t4[:], in0=bi[:, 0:512], in1=ks[:, 0, :], op=ALU.mult), reads=[bi, ks], writes=[t4])
                    P.op("pool", lambda e: e.tensor_tensor(out=Pbuf[:, 0, kg * 4:(kg + 1) * 4, :].rearrange("p k c -> p (k c)"), in0=t1[:], in1=t2[:], op=ALU.subtract),
                         reads=[t1, t2], djw=[Pbuf])
                    P.op("pool", lambda e: e.tensor_tensor(out=Pbuf[:, 1, kg * 4:(kg + 1) * 4, :].rearrange("p k c -> p (k c)"), in0=t3[:], in1=t4[:], op=ALU.add),
                         reads=[t3, t4], djw=[Pbuf])
                fft_fwd(64, F1d, mult)
                fft_inv(64)
                for g in range(128 // CG):
                    cs = slice(g * CG, (g + 1) * CG)
                    r0 = cc * 128 + g * CG
                    gx, gz = gxr.next(), gzr.next()
                    gd = [Pbuf] if g < 4 else []
                    P.dma("sp", gx[:], UC[(1 + n) * 512 + r0:(1 + n) * 512 + r0 + CG, 1:L + 1].rearrange("c (a b) -> a c b", b=128), reads=[UC], writes=[gx], djw=gd)
                    if n == 0:
                        P.dma("sp", gz[:], UC[r0:r0 + CG, 1:L + 1].rearrange("c (a b) -> a c b", b=128), reads=[UC], writes=[gz], djw=gd)
                    else:
                        P.dma("sp", gz[:], Z1F[r0:r0 + CG, :].rearrange("c (a b) -> a c b", b=128), reads=[Z1F], writes=[gz], djw=gd)
                    P.op("dve", lambda e, gz=gz, n=n, r0=r0: e.tensor_tensor(out=gz[:], in0=gz[:], in1=hbb[:, n, r0:r0 + CG].unsqueeze(2).to_broadcast([64, CG, 128]), op=ALU.mult),
                         reads=[gz, hbb, Pbuf], writes=[gz])
                    P.op("dve", lambda e, gz=gz, cs=cs: e.tensor_tensor(out=gz[:], in0=gz[:], in1=src[0:64, cs, :], op=ALU.add), reads=[gz, src, Pbuf], writes=[gz])
                    P.op("dve", lambda e, gz=gz, gx=gx: e.tensor_tensor(out=gz[:], in0=gz[:], in1=gx[:], op=ALU.mult), reads=[gz, gx, Pbuf], writes=[gz])
                    P.op("act", lambda e, gz=gz, cs=cs: e.activation(out=src[0:64, cs, :], in_=gz[:], func=AF.Copy), reads=[gz, Pbuf], djw=[srcg])
                    if n == 0:
                        P.dma("pool", Z1F[r0:r0 + CG, :].rearrange("c (a b) -> a c b", b=128), gz[:], reads=[gz, Pbuf], writes=[], djw=[Z1F])
                if n == 1:
                    P.dma("pool", YT[rows, :].rearrange("c (a b) -> a c b", b=128), src[0:64, :, :], reads=[src, srcg], writes=[], djw=[YT])
        P.pop_scope()

    if "C" in phases:
        P.push_scope()
        NC_ = L // 128
        lgf = P.sbuf("lgf", [128, 4], F32)
        lgb = P.sbuf("lgb", [128, 4], F32)
        P.dma("sp", lgf[:], lgf_d.h.partition_broadcast(128).rearrange("p o d -> p (o d)"), reads=[lgf_d], writes=[lgf])
        P.dma("sp", lgb[:], lgb_d.h.partition_broadcast(128).rearrange("p o d -> p (o d)"), reads=[lgb_d], writes=[lgb])
        rtab = P.sbuf("rtab_s", [128, 4, 128], F32)
        itab = P.sbuf("itab_s", [128, 2, 128], F32)
        jtab = P.sbuf("jtab_s", [128, 3], F32)
        P.dma("sp", rtab[:], rtab_d[:], reads=[rtab_d], writes=[rtab])
        P.dma("sp", itab[:], itab_d[:], reads=[itab_d], writes=[itab])
        P.dma("sp", jtab[:], jtab_d[:], reads=[jtab_d], writes=[jtab])
        flagc = P.sbuf("flagc", [128, 1], F32)
        P.dma("sp", flagc[:], flag_d[:], reads=[flag_d], writes=[flagc])
        omf = P.sbuf("omf", [128, 1], F32)
        P.op("dve", lambda e: e.tensor_scalar(out=omf[:], in0=flagc[:], scalar1=-1.0, scalar2=1.0, op0=ALU.mult, op1=ALU.add),
             reads=[flagc], writes=[omf])
        DmT = P.sbuf("DmT", [128, 4, 128], F32)
        etmp = P.sbuf("etmp", [128, 2, 128], F32)
        wq = P.sbuf("wq", [128, 4, 2, 128], F32)
        wk = P.sbuf("wk", [128, 4, 3], F32)
        gC = P.sbuf("gC", [128, 4, 2], F32)
        wkb = P.sbuf("wkb", [128, 4], F32)
        for h in range(4):
            P.op("act", lambda e, h=h: e.activation(out=etmp[:, 0, :], in_=rtab[:, 0, :], func=AF.Exp, scale=lgf[:, h:h + 1]), reads=[rtab, lgf], writes=[etmp])
            P.op("act", lambda e, h=h: e.activation(out=etmp[:, 1, :], in_=rtab[:, 1, :], func=AF.Exp, scale=lgb[:, h:h + 1]), reads=[rtab, lgb], writes=[etmp])
            P.op("dve", lambda e, h=h: e.tensor_tensor(out=etmp[:], in0=etmp[:], in1=rtab[:, 2:4, :], op=ALU.mult), reads=[etmp, rtab], writes=[etmp])
            P.op("dve", lambda e, h=h: e.tensor_tensor(out=DmT[:, h, :], in0=etmp[:, 0, :], in1=etmp[:, 1, :], op=ALU.add), reads=[etmp], writes=[DmT])
            P.op("act", lambda e, h=h: e.activation(out=wq[:, h, 0, :], in_=itab[:, 0, :], func=AF.Exp, scale=lgf[:, h:h + 1]), reads=[itab, lgf], writes=[wq])
            P.op("act", lambda e, h=h: e.activation(out=wq[:, h, 1, :], in_=itab[:, 1, :], func=AF.Exp, scale=lgb[:, h:h + 1]), reads=[itab, lgb], writes=[wq])
            P.op("act", lambda e, h=h: e.activation(out=wk[:, h, 0:1], in_=jtab[:, 0:1], func=AF.Exp, scale=lgf[:, h:h + 1]), reads=[jtab, lgf], writes=[wk])
            P.op("act", lambda e, h=h: e.activation(out=wkb[:, h:h + 1], in_=jtab[:, 1:2], func=AF.Exp, scale=lgb[:, h:h + 1]), reads=[jtab, lgb], writes=[wkb])
            P.op("act", lambda e, h=h: e.activation(out=gC[:, h, 0:1], in_=jtab[:, 2:3], func=AF.Exp, scale=lgf[:, h:h + 1]), reads=[jtab, lgf], writes=[gC])
            P.op("act", lambda e, h=h: e.activation(out=gC[:, h, 1:2], in_=jtab[:, 2:3], func=AF.Exp, scale=lgb[:, h:h + 1]), reads=[jtab, lgb], writes=[gC])
        kwb1 = P.sbuf("kwb", [128, NC_, 128], BF16)
        hd = [dict(qT=P.sbuf(f"qTs{i}", [128, NC_, 128], BF16), kT=P.sbuf(f"kTs{i}", [128, NC_, 128], BF16),
                   kwb=kwb1, vm=P.sbuf(f"vms{i}", [128, NC_, 128], BF16)) for i in range(2)]
        kwf = P.sbuf("kwf", [128, NC_, 128], BF16)
        RS = [P.sbuf("RSf", [128, NC_, 128], BF16), P.sbuf("RSb", [128, NC_, 128], BF16)]
        Rst = [[P.sbuf(f"R{d}{i}", [128, 128], F32) for i in range(2)] for d in range(2)]
        gmring = Ring([P.sbuf(f"gm{i}", [128, 4, 128], F32) for i in range(2)])
        qfring = Ring([P.sbuf(f"qf{i}", [128, 2, 4, 128], BF16) for i in range(2)])
        PTring = Ring([P.sbuf(f"PT{i}", [128, 4, 128], BF16) for i in range(3)])
        osring = Ring([P.sbuf(f"os{i}", [128, 4, 128], F32) for i in range(2)])
        sqring = Ring([P.sbuf(f"sq{i}", [128, 4, 128], F32) for i in range(2)])
        ybring = Ring([P.sbuf(f"yb{i}", [128, 4, 128], BF16) for i in range(3)])
        yTring = Ring([P.sbuf(f"yT{i}", [128, 512], BF16) for i in range(3)])
        g4ring = Ring([P.sbuf(f"g4{i}", [128, 8], F32) for i in range(4)])
        allb = Ring(banks)

        def load_head(h):
            d_ = hd[h % 2]
            P.dma("sp", d_["qT"][:].rearrange("p n t -> p (n t)"), QT[h], reads=[QT], writes=[d_["qT"]])
            P.dma("sp", d_["kT"][:].rearrange("p n t -> p (n t)"), KTs[h], reads=[KTs], writes=[d_["kT"]])
            P.dma("sp", d_["vm"][:], VM[h], reads=[VM], writes=[d_["vm"]])

        def load_km(h):
            P.dma("sp", kwb1[:], KM[h], reads=[KM], writes=[kwb1])

        load_head(0)
        load_km(0)
        for h in range(4):
            qT, kT, kwb, vm = (hd[h % 2][k_] for k_ in ("qT", "kT", "kwb", "vm"))
            if h + 1 < 4:
                load_head(h + 1)
            P.op("act", lambda e, h=h, kwb=kwb: e.activation(out=kwf[:], in_=kwb[:], func=AF.Copy, scale=wk[:, h, 0:1]), reads=[kwb, wk], writes=[kwf])
            P.op("dve", lambda e, h=h, kwb=kwb: e.tensor_scalar(out=kwb[:], in0=kwb[:], scalar1=wkb[:, h:h + 1], scalar2=None, op0=ALU.mult), reads=[kwb, wkb], writes=[kwb])
            P.op("pool", lambda e: e.memset(RS[0][:, 0, :], 0.0), writes=[RS[0]])
            P.op("pool", lambda e: e.memset(RS[1][:, NC_ - 1, :], 0.0), writes=[RS[1]])
            P.op("pool", lambda e: e.memset(Rst[0][0][:], 0.0), writes=[Rst[0][0]])
            P.op("pool", lambda e: e.memset(Rst[1][0][:], 0.0), writes=[Rst[1][0]])
            cur = [0, 0]
            for j in range(NC_ // 4):
                for d in range(2):
                    bk = allb.next()
                    kw = kwf if d == 0 else kwb
                    ns = [4 * j + c for c in range(4)] if d == 0 else [NC_ - 1 - (4 * j + c) for c in range(4)]
                    for c, n in enumerate(ns):
                        P.op("pe", lambda e, bk=bk, c=c, n=n, kw=kw, vm=vm: e.matmul(bk[:, c * 128:(c + 1) * 128], lhsT=kw[:, n, :], rhs=vm[:, n, :], start=True, stop=True),
                             reads=[kw, vm], writes=[bk], pe_acc=True)
                    for c, n in enumerate(ns):
                        tgt = n + 1 if d == 0 else n - 1
                        if tgt < 0 or tgt > NC_ - 1:
                            continue
                        ra, rb = Rst[d][cur[d] % 2], Rst[d][(cur[d] + 1) % 2]
                        cur[d] += 1
                        P.op("dve", lambda e, ra=ra, rb=rb, bk=bk, c=c, d=d, h=h: e.scalar_tensor_tensor(out=rb[:], in0=ra[:], scalar=gC[:, h, d:d + 1], in1=bk[:, c * 128:(c + 1) * 128],
                                                                                                   op0=ALU.mult, op1=ALU.add), reads=[ra, gC, bk], writes=[rb])
                        if (d == 0 and tgt == NC_ // 2) or (d == 1 and tgt == NC_ // 2 - 1):
                            P.op("dve", lambda e, rb=rb: e.tensor_scalar(out=rb[:], in0=rb[:], scalar1=omf[:, 0:1], scalar2=None, op0=ALU.mult), reads=[rb, omf], writes=[rb])
                        P.op("act", lambda e, rb=rb, d=d, tgt=tgt: e.activation(out=RS[d][:, tgt, :], in_=rb[:], func=AF.Copy), reads=[rb], djw=[RS[d]])

            if h + 1 < 4:
                load_km(h + 1)

            def stA(j, h=h, qT=qT, kT=kT):
                gm = gmring.next()
                P.dma("sp", gm[:], GM[j * 512:(j + 1) * 512, h * 128:(h + 1) * 128].rearrange("(n p) e -> p n e", p=128), reads=[GM], writes=[gm])
                qfb = qfring.next()
                P.op("dve", lambda e: e.tensor_tensor(out=qfb[:, 0, :, :], in0=qT[:, 4 * j:4 * j + 4, :], in1=wq[:, h, 0:1, :].to_broadcast([128, 4, 128]), op=ALU.mult),
                     reads=[qT, wq], writes=[qfb])
                P.op("dve", lambda e: e.tensor_tensor(out=qfb[:, 1, :, :], in0=qT[:, 4 * j:4 * j + 4, :], in1=wq[:, h, 1:2, :].to_broadcast([128, 4, 128]), op=ALU.mult),
                     reads=[qT, wq], djw=[qfb])
                pS = allb.next()
                for c in range(4):
                    n = 4 * j + c
                    P.op("pe", lambda e, c=c, n=n: e.matmul(pS[:, c * 128:(c + 1) * 128], lhsT=kT[:, n, :], rhs=qT[:, n, :], start=True, stop=True),
                         reads=[kT, qT], writes=[pS], pe_acc=True)
                PT = PTring.next()
                P.op("dve", lambda e: e.tensor_tensor(out=PT[:], in0=pS[:, 0:512].rearrange("p (c t) -> p c t", c=4),
                                                      in1=DmT[:, h:h + 1, :].to_broadcast([128, 4, 128]), op=ALU.mult), reads=[pS, DmT], writes=[PT])
                return gm, qfb, PT

            def stB(j, st, vm=vm):
                gm, qfb, PT = st
                po = allb.next()
                for c in range(4):
                    n = 4 * j + c
                    P.op("pe", lambda e, c=c, n=n: e.matmul(po[:, c * 128:(c + 1) * 128], lhsT=PT[:, c, :], rhs=vm[:, n, :], start=True, stop=False),
                         reads=[PT, vm], writes=[po], pe_acc=True)
                    P.op("pe", lambda e, c=c, n=n: e.matmul(po[:, c * 128:(c + 1) * 128], lhsT=qfb[:, 0, c, :], rhs=RS[0][:, n, :], start=False, stop=False),
                         reads=[qfb, RS[0]], writes=[po], pe_acc=True)
                    P.op("pe", lambda e, c=c, n=n: e.matmul(po[:, c * 128:(c + 1) * 128], lhsT=qfb[:, 1, c, :], rhs=RS[1][:, n, :], start=False, stop=True),
                         reads=[qfb, RS[1]], writes=[po], pe_acc=True)
                osb = osring.next()
                P.op("act", lambda e: e.activation(out=osb[:].rearrange("p c e -> p (c e)"), in_=po[:, 0:512], func=AF.Copy), reads=[po], writes=[osb])
                sq = sqring.next()
                P.op("act", lambda e: e.activation(out=sq[:].rearrange("p c e -> p (c e)"), in_=osb[:].rearrange("p c e -> p (c e)"), func=AF.Square), reads=[osb], writes=[sq])
                g4 = g4ring.next()
                P.op("dve", lambda e: e.tensor_reduce(out=g4[:, 0:4], in_=sq[:], axis=AX.X, op=ALU.add), reads=[sq], writes=[g4])
                P.op("act", lambda e: e.activation(out=g4[:, 4:8], in_=g4[:, 0:4], func=AF.Sqrt, bias=EPS, scale=1.0 / 128), reads=[g4], writes=[g4])
                P.op("dve", lambda e: e.reciprocal(out=g4[:, 4:8], in_=g4[:, 4:8]), reads=[g4], writes=[g4])
                P.op("dve", lambda e: e.tensor_tensor(out=sq[:], in0=osb[:], in1=g4[:, 4:8].unsqueeze(2).to_broadcast([128, 4, 128]), op=ALU.mult),
                     reads=[osb, g4], writes=[sq])
                yb = ybring.next()
                P.op("pool", lambda e: e.tensor_tensor(out=yb[:], in0=sq[:], in1=gm[:], op=ALU.mult), reads=[sq, gm], writes=[yb])
                return yb

            def stC(j, yb, h=h):
                bt = allb.next()
                btv = bt[:].bitcast(BF16)
                for c in range(4):
                    P.op("pe", lambda e, c=c: e.transpose(out=btv[:, c * 128:(c + 1) * 128], in_=yb[:, c, :], identity=ident[:]),
                         reads=[yb, ident], writes=[bt], pe_acc=True)
                yT = yTring.next()
                P.op("act", lambda e: e.activation(out=yT[:], in_=btv[:, 0:512], func=AF.Copy), reads=[bt], writes=[yT])
                P.dma("pool", YT[512 + h * 128:512 + (h + 1) * 128, j * 512:(j + 1) * 512], yT[:], reads=[yT], writes=[], djw=[YT])

            NG = NC_ // 4
            sA = {0: stA(0)}
            sB = {}
            for j in range(NG + 1):
                if j + 1 < NG:
                    sA[j + 1] = stA(j + 1)
                if j < NG:
                    sB[j] = stB(j, sA.pop(j))
                if j >= 1:
                    stC(j - 1, sB.pop(j - 1))
        P.pop_scope()

    if "D" in phases:
        P.push_scope()
        alloc_ffn()
        if "A" not in phases:
            cast_weights(1)
        gpm = load_gpost("mix_post_g", False)
        gp2 = load_gpost("ffn2_post_g", True)
        load_w2(1)
        wouts = P.sbuf("wouts", [128, 8, D], BF16)
        for k in range(8):
            P.dma("pool", wouts[:, k, :], wout_d[:, k * D:(k + 1) * D], reads=[wout_d], writes=[wouts])
        ymring = Ring([P.sbuf(f"ym{i}", [128, 8, T], BF16) for i in range(2)])
        ycnt = [0]

        def s0(b):
            ym = ymring.next()
            P.dma("sp", ym[:], YT[:, b * T:(b + 1) * T].rearrange("(k p) t -> p k t", p=128), reads=[YT], writes=[ym])
            st = string.next()
            hns = []
            for tt in range(TT):
                ys = ysets[ycnt[0] % 2]
                ycnt[0] += 1
                mm_tokmajor(ys, [ym[:, k, tt * 128:(tt + 1) * 128] for k in range(8)],
                            lambda i, dh: wouts[:, i, dh * 512:(dh + 1) * 512], lambda i: [ym, wouts])
                xt = xring.next()
                r0 = b * T + tt * 128
                P.dma("sp", xt[:], X1[r0:r0 + 128, :], reads=[X1], writes=[xt])
                xo = xoring.next()
                epilogue(ys, xt, gpm, xo)
                P.dma("pool", X2[r0:r0 + 128, :], xo[:], reads=[xo], writes=[], djw=[X2])
                prep_tile(xo, st, tt, None)
                rstd_cols(st, tt, 1)
                hns.append(norm_cast(xo, st, tt))
            return hns

        hT_cur = prep_pe(s0(0), "ffn2_pre_g")
        for b in range(nb):
            ffn_s3(hT_cur, w13b[1])
            if b + 1 < nb:
                hns_next = s0(b + 1)
            for tt in range(TT):
                if tt == TT - 1 and b + 1 < nb:
                    hT_cur = prep_pe(hns_next, "ffn2_pre_g")
                ys = ysets[ycnt[0] % 2]
                ycnt[0] += 1
                mm_tokmajor(ys, [aT[fc][:, tt * 128:(tt + 1) * 128] for fc in range(KF)],
                            lambda i, dh: w2s[:, i, dh * 512:(dh + 1) * 512], lambda i: [aT[i], w2s])
                xt = xring.next()
                r0 = b * T + tt * 128
                P.dma("sp", xt[:], X2[r0:r0 + 128, :], reads=[X2], writes=[xt])
                xo = xoring.next()
                epilogue(ys, xt, gp2, xo)
                P.dma("pool", out_d[r0:r0 + 128, :], xo[:], reads=[xo], writes=[], djw=[out_d])
        P.pop_scope()

    P.finish_all("sp")
    P.emit()
    return nc, P


def _shared_maps(inp):
    f = np.float32
    m = {}
    for i, nm in ((1, "ffn1"), (2, "ffn2")):
        w1 = np.asarray(inp[nm + "_w1"][0], f)
        w3 = np.asarray(inp[nm + "_w3"][0], f)
        w2 = np.asarray(inp[nm + "_w2"][0], f)
        a = np.stack([w1, w3], 0).reshape(2, KD, 128, KF, 128)
        m[f"w13_{i}"] = np.ascontiguousarray(a.transpose(3, 2, 0, 1, 4)).reshape(KF, 128, 2 * KD * 128)
        m[f"w2_{i}"] = np.ascontiguousarray(w2.reshape(KF, 128, D).transpose(1, 0, 2)).reshape(128, KF * D)
    win = np.asarray(inp["w_in"][0], f)
    cols = [win[:, c * 128:(c + 1) * 128] for c in range(NHC)]
    cols += [win[:, 1536 + h * 128:1536 + (h + 1) * 128] for h in range(4)]
    cols += [win[:, 2048 + h * 128:2048 + (h + 1) * 128] for h in range(4)]
    fm = np.stack(cols, 0).reshape(20, KD, 128, 128).transpose(0, 2, 1, 3)
    m["win_fm"] = np.ascontiguousarray(fm).reshape(20, 128, KD * 128)
    tm = np.stack([win[:, 2560:3072], win[:, 3072:3584]], 0).reshape(2, KD, 128, 512).transpose(0, 2, 1, 3)
    m["win_tm"] = np.ascontiguousarray(tm).reshape(2, 128, KD * 512)
    m["wout"] = np.ascontiguousarray(np.asarray(inp["w_out"][0], f).reshape(8, 128, D).transpose(1, 0, 2)).reshape(128, 8 * D)
    for k in ("ffn1_pre_g", "mix_pre_g", "ffn2_pre_g"):
        m[k] = np.ascontiguousarray(np.asarray(inp[k][0], f).reshape(KD, 128).T)
    for k in ("ffn1_post_g", "mix_post_g", "ffn2_post_g"):
        m[k] = np.asarray(inp[k][0], f).reshape(1, D)
    sw = np.asarray(inp["short_w"][0], f)
    m["short_w"] = np.ascontiguousarray(sw.reshape(3, NHC, 128).transpose(2, 1, 0))
    m["short_b"] = np.ascontiguousarray(np.asarray(inp["short_b"][0], f).reshape(NHC, 128).T)
    m["ident"] = np.eye(128, dtype=np.float32).astype(ml_dtypes.bfloat16)
    pm = np.zeros((128, 128), f)
    for i in range(128):
        pm[(i + 64) % 128, i] = 1.0
    m["pswap"] = pm
    dl = np.linspace(math.log(1e-2) / 1.5, math.log(1e-2) / 0.3, 512, dtype=f)
    m["negd"] = np.ascontiguousarray((-np.abs(dl)).reshape(4, 128).T)
    for kk in ("filt_w1", "filt_w2", "filt_w3", "filt_w4"):
        m[kk] = np.ascontiguousarray(np.asarray(inp[kk][0], f))
    m["filt_b"] = np.ascontiguousarray(np.stack([np.asarray(inp[kk][0], f) for kk in ("filt_b1", "filt_b2", "filt_b3")], 1))
    m["filt_freq"] = np.asarray(inp["filt_freq"][0], f).reshape(64, 1)
    m["hbias_row"] = np.ascontiguousarray(np.asarray(inp["hyena_bias"][0], f).reshape(1, 1024))
    m["ret_log_decay_f"] = np.asarray(inp["ret_log_decay_f"], f).reshape(1, 4)
    m["ret_log_decay_b"] = np.asarray(inp["ret_log_decay_b"], f).reshape(1, 4)
    si = np.arange(128)[:, None]
    ti = np.arange(128)[None, :]
    diff = (ti - si).astype(f)
    m["rtab"] = np.ascontiguousarray(np.stack([np.maximum(diff, 0), np.maximum(-diff, 0), (diff >= 0).astype(f), (diff < 0).astype(f)], 1))
    ii = np.arange(128, dtype=f)
    m["itab"] = np.ascontiguousarray(np.broadcast_to(np.stack([ii + 1, 128 - ii], 0)[None], (128, 2, 128)))
    m["jtab"] = np.ascontiguousarray(np.stack([127 - ii, ii, np.full(128, 128.0, f)], 1))
    return m


def _core_tables(seq_len):
    f = np.float32
    m = {}
    pos = (np.arange(L) % seq_len).astype(f)
    inv = (1.0 / (10000.0 ** (np.arange(0, 128, 2, dtype=f) / f(128)))).astype(f)
    ang = (pos[:, None] * inv[None, :]).astype(f)
    c = np.cos(ang).astype(f)
    s = np.sin(ang).astype(f)
    cosT = np.concatenate([c, c], 1).T
    sinT = np.concatenate([-s, s], 1).T
    rot = np.stack([cosT, sinT], 1).reshape(128, 2, NB, T).transpose(2, 0, 1, 3)
    m["rot"] = np.ascontiguousarray(rot, dtype=f)
    m["pflag"] = np.full((128, 1), 0.0 if seq_len == L else 1.0, f)
    lag = np.arange(16384)
    mm = np.where(lag < 8192, lag, 16384 - lag)
    valid = np.where(lag < 8192, mm < seq_len, (mm >= 1) & (mm <= seq_len - 1))
    mi = np.where(valid, mm, 0)
    tl = np.linspace(0.0, 1.0, seq_len, dtype=f)
    wl = (f(2.0 * math.pi) * np.arange(seq_len, dtype=f) / f(seq_len)).astype(f)
    fbv = np.linspace(1e-4, 15, 16, dtype=f)
    zfull = np.concatenate([tl[:, None], np.cos(fbv[None, :] * wl[:, None]), -np.sin(fbv[None, :] * wl[:, None])], 1).astype(f)
    zt_ = np.ascontiguousarray(zfull[mi].T)
    zhi = zt_.astype(ml_dtypes.bfloat16)
    zlo = (zt_ - zhi.astype(f)).astype(ml_dtypes.bfloat16)
    m["zext"] = np.ascontiguousarray(np.stack([zhi, zlo], 1))
    m["text"] = np.where(valid, tl[mi], f(1e4)).astype(f).reshape(1, 16384)
    amap = np.arange(64) if seq_len == L else np.concatenate([np.arange(32), 64 + np.arange(32)])
    k = np.arange(128)
    bf = ml_dtypes.bfloat16
    th = 2 * np.pi * np.outer(amap, k) / 128.0
    m["F1d"] = np.concatenate([np.cos(th), -np.sin(th)], 1).astype(bf)
    th = 2 * np.pi * np.outer(np.arange(128), k) / 128.0
    m["F1k"] = np.concatenate([np.cos(th), -np.sin(th)], 1).astype(bf)
    m["F3"] = np.ascontiguousarray(np.stack([np.concatenate([np.cos(th), np.sin(th)], 1), np.concatenate([-np.sin(th), np.cos(th)], 1)], 1)).astype(bf)
    b_ = np.arange(128)
    phi = 2 * np.pi * (b_[None, :, None] * k[None, None, :] / 128.0 + b_[None, :, None] * k[:, None, None] / 16384.0)
    m["GT"] = np.ascontiguousarray(np.concatenate([np.cos(phi), -np.sin(phi), np.sin(phi)], 2)).astype(bf)
    psi = 2 * np.pi * (amap[None, None, :] * k[None, :, None] / 128.0 + b_[:, None, None] * k[None, :, None] / 16384.0)
    wk1 = np.where((k == 0) | (k == 64), 1.0, np.where(k < 64, 2.0, 0.0))[None, :, None]
    m["GI"] = np.ascontiguousarray(np.concatenate([np.cos(psi), -np.sin(psi)], 2) * wk1 / 16384.0).astype(bf)
    return m


_CACHE = {}


def kernel(**inputs):
    inp = {k: np.asarray(v) for k, v in inputs.items()}
    xs = np.asarray(inp["x_sample"], np.float32)
    xp = np.asarray(inp["x_prompt"], np.float32)
    shared = _shared_maps(inp)
    tab_s = _core_tables(L)
    tab_p = _core_tables(L // 2)
    in_maps = []
    for c in range(8):
        m = dict(shared)
        if c < 4:
            m.update(tab_s)
            m["x"] = np.ascontiguousarray(xs[c])
        else:
            j = min(c - 4, 1)
            m.update(tab_p)
            m["x"] = np.ascontiguousarray(np.concatenate([xp[2 * j], xp[2 * j + 1]], 0))
        in_maps.append(m)
    if "nc" not in _CACHE:
        _CACHE["nc"] = build()[0]
    res = run_bass_kernel_spmd(_CACHE["nc"], in_maps, core_ids=list(range(8)))
    outs = [np.asarray(r["out"], np.float32) for r in res.results]
    y_sample = np.stack(outs[0:4], 0)
    y_prompt = np.stack([outs[4][:L // 2], outs[4][L // 2:], outs[5][:L // 2], outs[5][L // 2:]], 0)
    return (y_prompt, y_sample)
```
